# Optimizing a Trainium2 kernel written in Bass

```python
import jax, jax.numpy as jnp
from jax import lax
import numpy as np

D_MODEL = 1024
BATCH = 2
SEQ = 8192
DEPTH = 2

CTX_LEN = 256
GRID_W = 64
W_CONV = D_MODEL
W_LRU = D_MODEL
LRU_BLOCKS = 8
LRU_BW = W_LRU // LRU_BLOCKS
CONV_A_WIDTH = 3
CONV_B_WIDTH = 4
LRU_C = 8.0
RMS_EPS = 1e-6
PROJ_WIDTHS = (W_CONV, W_CONV, W_CONV, W_CONV, W_LRU, W_LRU, D_MODEL, D_MODEL)
SPLIT_POINTS = tuple(int(v) for v in np.cumsum(PROJ_WIDTHS)[:-1])
D_IN = int(sum(PROJ_WIDTHS))

kernel_name = "hybrid_shortconv_rglru_dit_block"


def rmsnorm(x, g):
    xf = x.astype(jnp.float32)
    y = xf * lax.rsqrt(jnp.mean(xf * xf, axis=-1, keepdims=True) + RMS_EPS)
    return (y * g.astype(jnp.float32)).astype(x.dtype)


def dwconv(v, w, pad_left):
    k = w.shape[0]
    n = v.shape[-2]
    pad = [(0, 0)] * (v.ndim - 2) + [(pad_left, k - 1 - pad_left), (0, 0)]
    vp = jnp.pad(v, pad)
    out = w[0] * lax.slice_in_dim(vp, 0, n, axis=v.ndim - 2)
    for j in range(1, k):
        out = out + w[j] * lax.slice_in_dim(vp, j, j + n, axis=v.ndim - 2)
    return out


def grid_conv(v, w, pad_left):
    b, n, ch = v.shape
    rows = n // GRID_W
    return dwconv(v.reshape(b, rows, GRID_W, ch), w, pad_left).reshape(b, n, ch)


def block_diag(x, w, bias):
    xb = x.reshape(x.shape[:-1] + (LRU_BLOCKS, LRU_BW))
    y = jnp.einsum("blnk,nkj->blnj", xb, w)
    return y.reshape(x.shape) + bias


def _combine(lhs, rhs):
    a1, b1 = lhs
    a2, b2 = rhs
    return a1 * a2, a2 * b1 + b2


def linear_scan(a, u, h0, reverse):
    if reverse:
        a = jnp.flip(a, axis=1)
        u = jnp.flip(u, axis=1)
    u = u.at[:, 0].add(a[:, 0] * h0)
    _, h = lax.associative_scan(_combine, (a, u), axis=1)
    if reverse:
        h = jnp.flip(h, axis=1)
    return h


def rglru_direction(xc, h0, wr, br, wi, bi, lam, reverse):
    r = jax.nn.sigmoid(block_diag(xc, wr, br).astype(jnp.float32))
    i = jax.nn.sigmoid(block_diag(xc, wi, bi).astype(jnp.float32))
    log_a = -LRU_C * r * jax.nn.softplus(-lam.astype(jnp.float32))
    a = jnp.exp(log_a)
    mult = jnp.sqrt(-jnp.expm1(2.0 * log_a))
    u = mult * i * xc.astype(jnp.float32)
    h = linear_scan(a, u, h0, reverse)
    final = h[:, 0] if reverse else h[:, -1]
    return h, final


def bidir_rglru(xc, h0s, wr, br, wi, bi, lam):
    h_f, fin_f = rglru_direction(xc, h0s[0], wr[0], br[0], wi[0], bi[0], lam[0], False)
    h_b, fin_b = rglru_direction(xc, h0s[1], wr[1], br[1], wi[1], bi[1], lam[1], True)
    return (h_f + h_b).astype(xc.dtype), (fin_f, fin_b)


def gated_merge(chunks, y_lru, conv_fn, conv3_w, w_out_a, w_out_b, w_o):
    v, g_b, g_c, z_a, _, z_b, m_a, m_b = chunks
    y_a = g_b * conv_fn(g_c * v, conv3_w, 1) * jax.nn.silu(z_a)
    y_b = y_lru * jax.nn.silu(z_b)
    merged = jax.nn.sigmoid(m_a) * (y_a @ w_out_a) + jax.nn.sigmoid(m_b) * (y_b @ w_out_b)
    return merged @ w_o


def setup_inputs(seed: int = 0) -> dict:
    key = jax.random.key(seed)
    ks = jax.random.split(key, 20)
    f32 = jnp.float32
    nrm = lambda k, shape, s: jax.random.normal(k, shape, f32) * s
    x = nrm(ks[0], (BATCH, SEQ, D_MODEL), 1.0)
    c = nrm(ks[1], (BATCH, D_MODEL), 1.0)
    ctx = nrm(ks[2], (BATCH, CTX_LEN, D_MODEL), 1.0)
    c_ctx = nrm(ks[3], (D_MODEL,), 1.0)
    w_ada = nrm(ks[4], (DEPTH, D_MODEL, 3 * D_MODEL), D_MODEL ** -0.5)
    b_ada = nrm(ks[5], (DEPTH, 3 * D_MODEL), 0.02)
    norm_g = 1.0 + nrm(ks[6], (DEPTH, D_MODEL), 0.05)
    w_in = nrm(ks[7], (DEPTH, D_MODEL, D_IN), D_MODEL ** -0.5)
    conv3_w = nrm(ks[8], (DEPTH, CONV_A_WIDTH, W_CONV), CONV_A_WIDTH ** -0.5)
    conv4_w = nrm(ks[9], (DEPTH, CONV_B_WIDTH, W_LRU), CONV_B_WIDTH ** -0.5)
    conv4_b = nrm(ks[10], (DEPTH, W_LRU), 0.02)
    lru_wr = nrm(ks[11], (DEPTH, 2, LRU_BLOCKS, LRU_BW, LRU_BW), LRU_BW ** -0.5)
    lru_br = nrm(ks[12], (DEPTH, 2, W_LRU), 0.02)
    lru_wi = nrm(ks[13], (DEPTH, 2, LRU_BLOCKS, LRU_BW, LRU_BW), LRU_BW ** -0.5)
    lru_bi = nrm(ks[14], (DEPTH, 2, W_LRU), 0.02)
    a_pow = jax.random.uniform(ks[15], (DEPTH, 2, W_LRU), f32, 0.9, 0.999)
    base = a_pow ** (1.0 / LRU_C)
    lru_lambda = jnp.log(base) - jnp.log1p(-base)
    w_out_a = nrm(ks[16], (DEPTH, W_CONV, D_MODEL), W_CONV ** -0.5)
    w_out_b = nrm(ks[17], (DEPTH, W_LRU, D_MODEL), W_LRU ** -0.5)
    w_o = nrm(ks[18], (DEPTH, D_MODEL, D_MODEL), D_MODEL ** -0.5)
    final_g = 1.0 + nrm(ks[19], (D_MODEL,), 0.05)
    return {"x": x, "c": c, "ctx": ctx, "c_ctx": c_ctx, "w_ada": w_ada, "b_ada": b_ada,
            "norm_g": norm_g, "w_in": w_in, "conv3_w": conv3_w, "conv4_w": conv4_w,
            "conv4_b": conv4_b, "lru_wr": lru_wr, "lru_br": lru_br, "lru_wi": lru_wi,
            "lru_bi": lru_bi, "lru_lambda": lru_lambda, "w_out_a": w_out_a,
            "w_out_b": w_out_b, "w_o": w_o, "final_g": final_g}


def reference(x, c, ctx, c_ctx, w_ada, b_ada, norm_g, w_in, conv3_w, conv4_w, conv4_b,
              lru_wr, lru_br, lru_wi, lru_bi, lru_lambda, w_out_a, w_out_b, w_o, final_g):
    n_batch = x.shape[0]
    silu_c = jax.nn.silu(c)
    silu_cc = jax.nn.silu(c_ctx)
    for l in range(DEPTH):
        last = l == DEPTH - 1
        shift, scale, gate = jnp.split(silu_c @ w_ada[l] + b_ada[l], 3, axis=-1)
        shift_c, scale_c, gate_c = jnp.split(silu_cc @ w_ada[l] + b_ada[l], 3, axis=-1)
        h = rmsnorm(x, norm_g[l]) * (1.0 + scale[:, None]) + shift[:, None]
        hc = rmsnorm(ctx, norm_g[l]) * (1.0 + scale_c) + shift_c
        chunks = jnp.split(h @ w_in[l], SPLIT_POINTS, axis=-1)
        chunks_c = jnp.split(hc @ w_in[l], SPLIT_POINTS, axis=-1)
        lru_p = (lru_wr[l], lru_br[l], lru_wi[l], lru_bi[l], lru_lambda[l])
        xc_c = dwconv(chunks_c[4], conv4_w[l], 2) + conv4_b[l]
        zeros = jnp.zeros((n_batch, W_LRU), jnp.float32)
        y_lru_c, finals = bidir_rglru(xc_c, (zeros, zeros), *lru_p)
        xc = grid_conv(chunks[4], conv4_w[l], 2) + conv4_b[l]
        y_lru, _ = bidir_rglru(xc, finals, *lru_p)
        x = x + gate[:, None] * gated_merge(chunks, y_lru, grid_conv, conv3_w[l],
                                            w_out_a[l], w_out_b[l], w_o[l])
        if not last:
            ctx = ctx + gate_c * gated_merge(chunks_c, y_lru_c, dwconv, conv3_w[l],
                                             w_out_a[l], w_out_b[l], w_o[l])
    return rmsnorm(x, final_g)
```

```python
import contextlib
import numpy as np
import concourse.bass as bass
import concourse.mybir as mybir
from concourse.bass_utils import run_bass_kernel_spmd

F32 = mybir.dt.float32
BF16 = mybir.dt.bfloat16
AF = mybir.ActivationFunctionType
ALU = mybir.AluOpType

D = 1024
NB = 2
SEQ = 8192
DEPTH = 2
CTX = 256
GW = 64
NCORES = 8
QPB = 4
LAT = SEQ // QPB
NT = CTX + LAT
KC = 8
RMS_EPS = 1e-6
TILES = [(0, 256, 256)] + [(CTX + 512 * i, 512, GW) for i in range(4)]
NSLOT = 8
NTEMP = 10

_VL = {}
_off = 0
for _l in range(DEPTH):
    for _name, _n in (("g", 8), ("bada", 24), ("w3", 24), ("w4", 32), ("b4", 8), ("br", 16), ("bi", 16), ("lam", 16)):
        _VL[(_name, _l)] = _off
        _off += _n
for _name, _n in (("fg", 8), ("c0", 8), ("c1", 8), ("cctx", 8), ("maskf", 8), ("maskb", 8), ("one", 1), ("eps", 1),
                  ("zero", 1), ("quarter", 1), ("sel0", 1), ("sel1", 1)):
    _VL[_name] = _off
    _off += _n
NV = _off

LD = 128
DER_SCC = DEPTH * LD
NDER = DER_SCC + 24


def _lay(v):
    v = np.asarray(v, dtype=np.float32).reshape(-1, 8, 128)
    return np.ascontiguousarray(v.transpose(2, 0, 1).reshape(128, -1))


def I(method, **kw):
    return [(method, kw)]


class Sched:
    ENGS = ("pe", "act", "dve", "pool", "sync")

    def __init__(self):
        self.q = {e: [] for e in self.ENGS}
        self.cnt = {e: 0 for e in self.ENGS}
        self.lw = {}
        self.rd = {}
        self.seen = {e: {} for e in self.ENGS}
        self.dcum = {}
        self.same_engine_sync = True

    def _waits(self, eng, reads, writes):
        toks = []
        for k in reads:
            t = self.lw.get(k)
            if t is not None:
                toks.append(t)
        for k in writes:
            t = self.lw.get(k)
            if t is not None:
                toks.append(t)
            toks.extend(self.rd.get(k, ()))
        need = {}
        for s, v in toks:
            if s == eng and (eng == "pe" or not self.same_engine_sync):
                continue
            if need.get(s, 0) < v:
                need[s] = v
        out = []
        for s, v in need.items():
            if self.seen[eng].get(s, 0) >= v:
                continue
            self.seen[eng][s] = v
            out.append((s, v))
        return out

    def _commit(self, tok, reads, writes):
        for k in writes:
            self.lw[k] = tok
            self.rd[k] = []
        for k in reads:
            if k not in writes:
                self.rd.setdefault(k, []).append(tok)

    def op(self, eng, fn, reads=(), writes=()):
        reads, writes = list(reads), list(writes)
        w = self._waits(eng, reads, writes)
        self.cnt[eng] += 1
        tok = (eng, self.cnt[eng])
        self.q[eng].append((w, fn, (eng, 1)))
        self._commit(tok, reads, writes)
        return tok

    def dma(self, queue, fn, reads, writes, sem, n=1, inc=16):
        reads, writes = list(reads), list(writes)
        w = self._waits(queue, reads, writes)
        self.dcum[sem] = self.dcum.get(sem, 0) + inc * n
        tok = (sem, self.dcum[sem])
        self.q[queue].append((w, fn, (sem, inc)))
        self._commit(tok, reads, writes)
        return tok

    def barrier(self):
        toks = [(e, c) for e, c in self.cnt.items() if c > 0] + list(self.dcum.items())
        for eng in self.ENGS:
            w = []
            for s, v in toks:
                if s == eng:
                    continue
                if self.seen[eng].get(s, 0) >= v:
                    continue
                self.seen[eng][s] = v
                w.append((s, v))
            if w:
                self.q[eng].append((w, None, None))
        self.lw.clear()
        self.rd.clear()


def build_nc(depth=DEPTH, final_norm=True):
    nc = bass.Bass("TRN2", target_bir_lowering=False)
    S = Sched()

    xT = nc.dram_tensor("xT", [D, NT], F32, kind="ExternalInput").ap()
    vec_d = nc.dram_tensor("vec", [128, NV], F32, kind="ExternalInput").ap()
    w_ada_s = nc.dram_tensor("w_ada_s", [DEPTH, D, 384], F32, kind="ExternalInput").ap()
    w_in_d = nc.dram_tensor("w_in", [DEPTH, D, 8 * D], F32, kind="ExternalInput").ap()
    wr_d = nc.dram_tensor("lru_wr", [DEPTH, 2, 8, 128, 128], F32, kind="ExternalInput").ap()
    wi_d = nc.dram_tensor("lru_wi", [DEPTH, 2, 8, 128, 128], F32, kind="ExternalInput").ap()
    woa_d = nc.dram_tensor("w_out_a", [DEPTH, D, D], F32, kind="ExternalInput").ap()
    wob_d = nc.dram_tensor("w_out_b", [DEPTH, D, D], F32, kind="ExternalInput").ap()
    wo_d = nc.dram_tensor("w_o", [DEPTH, D, D], F32, kind="ExternalInput").ap()
    outT = nc.dram_tensor("outT", [D, LAT], F32, kind="ExternalOutput").ap()
    xs = nc.dram_tensor("xs", [D, NT], F32).ap()
    cc_ada_in = nc.dram_tensor("cc_ada_in", [128, 18], F32)
    cc_ada_out = nc.dram_tensor("cc_ada_out", [NCORES * 128, 18], F32)
    cc_in = [[nc.dram_tensor(f"ccin_{l}_{j}", [128, 4], F32) for j in range(8)] for l in range(DEPTH)]
    cc_out = [[nc.dram_tensor(f"ccout_{l}_{j}", [NCORES * 128, 4], F32) for j in range(8)] for l in range(DEPTH)]

    es = contextlib.ExitStack()
    with es:
        REG_H = es.enter_context(nc.sbuf_tensor("reg_h", [128, 8 * NT], BF16))
        REG_M = es.enter_context(nc.sbuf_tensor("reg_m", [128, 8 * NT], BF16))
        BIG = es.enter_context(nc.sbuf_tensor("reg_big", [128, 21888], F32))
        WR = es.enter_context(nc.sbuf_tensor("wring", [128, NSLOT * 1024], BF16))
        TMP = es.enter_context(nc.sbuf_tensor("tmps", [128, NTEMP * 512], F32))
        VEC = es.enter_context(nc.sbuf_tensor("vecs", [128, NV], F32))
        DER = es.enter_context(nc.sbuf_tensor("der", [128, NDER], F32))
        SM = es.enter_context(nc.sbuf_tensor("sm", [128, 256], F32))
        ADG = es.enter_context(nc.sbuf_tensor("adg", [128, NCORES * 18], F32))
        JUNK = es.enter_context(nc.sbuf_tensor("junk", [128, LAT], F32))
        ONES = es.enter_context(nc.sbuf_tensor("ones", [128, 128], F32))
        PS = es.enter_context(nc.psum_tensor("ps", [128, 8, 512], F32))

        H = REG_H[:, :].rearrange("p (k t) -> p k t", k=8)
        MG = REG_M[:, :].rearrange("p (k t) -> p k t", k=8)
        MF = REG_M[:, :].bitcast(F32)
        ADAST = MF[:, 0:6144].rearrange("p (l k n) -> p l k n", l=DEPTH, k=8)
        X = BIG[:, 0:8 * NT].rearrange("p (k t) -> p k t", k=8)
        YB = BIG[:, 0:9216].bitcast(BF16).rearrange("p (k t) -> p k t", k=8)
        YA = BIG[:, 9216:18432].bitcast(BF16).rearrange("p (k t) -> p k t", k=8)
        A_ = [BIG[:, 9216:11520], BIG[:, 11520:13824]]
        U_ = [BIG[:, 13824:16128], BIG[:, 16128:18432]]
        XC = BIG[:, 18432:20736]
        XCB = BIG[:, 20736:21888].bitcast(BF16)
        WS = WR[:, :].rearrange("p (s k m) -> p s k m", s=NSLOT, k=8)

        def vcol(name, i=0, n=1):
            o = _VL[name] + i
            return VEC[:, o:o + n]

        def dcol(o, n=1):
            return DER[:, o:o + n]

        def mod_col(l, which, kc, n):
            return dcol(l * LD + n * 24 + which * 8 + kc)

        def gs_col(l, kc, n):
            return dcol(l * LD + 48 + n * 8 + kc)

        def c1_col(l, d, j):
            return dcol(l * LD + 64 + d * 8 + j)

        def c2_col(l, d, j):
            return dcol(l * LD + 80 + d * 8 + j)

        PKS = [SM[:, 0:4], SM[:, 202:206]]
        G3S = [SM[:, 4:36].rearrange("p (r f) -> p r f", f=4), SM[:, 206:238].rearrange("p (r f) -> p r f", f=4)]
        AD18 = SM[:, 160:178]
        T24 = SM[:, 178:202]
        LSETS = [dict(A=[BIG[:, 9216:11520], BIG[:, 11520:13824]], U=[BIG[:, 13824:16128], BIG[:, 16128:18432]]),
                 dict(A=[MF[:, 0:2304], MF[:, 2304:4608]], U=[MF[:, 4608:6912], MF[:, 6912:9216]])]
        PE_ = [SM[:, 36:44], SM[:, 52:60]]
        FE_ = [SM[:, 44:52], SM[:, 60:68]]
        FOLD = [SM[:, 68:76], SM[:, 76:84]]
        RACC = [SM[:, 84:88], SM[:, 88:92]]
        RSUM = [SM[:, 92:93], SM[:, 93:94]]
        JUNK4 = SM[:, 94:98]
        T8 = SM[:, 102:110]
        T16A = SM[:, 110:126]
        T16B = SM[:, 126:142]

        st = {"ps": 0, "t": 0}

        def psalloc():
            b = st["ps"]
            st["ps"] = (b + 1) % 8
            return b

        def talloc():
            s = st["t"]
            st["t"] = (s + 1) % NTEMP
            return s

        def tap(s, N):
            return TMP[:, s * 512:s * 512 + N]

        wplan = []
        for l in range(depth):
            for j in range(9):
                if j < 8:
                    wplan += [("in", l, 4, j), ("gate", l, j)]
                if j > 0:
                    wplan += [("in", l, 5, j - 1)]
            for j in range(8):
                wplan += [("in", l, 0, j), ("in", l, 2, j), ("in", l, 1, j), ("in", l, 3, j)]
            for j in range(8):
                wplan += [("in", l, 6, j), ("in", l, 7, j), ("oa", l, j), ("ob", l, j)]
            for j in range(8):
                wplan += [("o", l, j)]
        wst = {"issued": 0, "consumed": 0, "released": 0}

        def issue_wload(idx):
            spec = wplan[idx]
            slot = idx % NSLOT
            kind = spec[0]
            if kind == "gate":
                _, l, j = spec
                srcs = [(WS[:, slot, 0:2, :], wr_d[l, :, j].rearrange("d k m -> k d m")),
                        (WS[:, slot, 2:4, :], wi_d[l, :, j].rearrange("d k m -> k d m"))]
            else:
                if kind == "in":
                    _, l, c, j = spec
                    src = w_in_d[l].rearrange("(kc p) n -> p kc n", p=128)[:, :, c * D + j * 128:c * D + (j + 1) * 128]
                else:
                    _, l, j = spec
                    base = {"oa": woa_d, "ob": wob_d, "o": wo_d}[kind]
                    src = base[l].rearrange("(kc p) n -> p kc n", p=128)[:, :, j * 128:(j + 1) * 128]
                srcs = [(WS[:, slot, :, :], src)]
            for i, (dst, src) in enumerate(srcs):
                S.dma("pool", (I("dma_start", out=dst, in_=src)),
                      reads=[], writes=[("w", slot)] if i == 0 else [], sem=f"w{slot}")
            if len(srcs) > 1:
                S.lw[("w", slot)] = (f"w{slot}", S.dcum[f"w{slot}"])

        def wpump():
            lim = min(len(wplan), wst["released"] + NSLOT)
            while wst["issued"] < lim:
                issue_wload(wst["issued"])
                wst["issued"] += 1

        def wacquire(spec):
            c = wst["consumed"]
            assert wplan[c] == spec, (wplan[c], spec)
            assert c < wst["released"] + NSLOT
            wpump()
            wst["consumed"] = c + 1
            return c % NSLOT

        def wrelease(n):
            wst["released"] += n
            assert wst["released"] <= wst["consumed"]
            wpump()

        def mm8(bank, N, slot, rhs_fn, rkeys):
            ins = []
            for kc in range(8):
                ins += I("matmul", out=PS[:, bank, 0:N], lhsT=WS[:, slot, kc, :], rhs=rhs_fn(kc),
                         start=(kc == 0), stop=(kc == 7))
            S.op("pe", ins, reads=[("w", slot)] + rkeys, writes=[("ps", bank)])

        S.dma("sync", I("dma_start", out=VEC[:, :], in_=vec_d), reads=[], writes=[("vec",)], sem="vecl")
        S.dma("sync", I("dma_start", out=ADAST, in_=w_ada_s.rearrange("l (kc p) n -> p l kc n", p=128)),
              reads=[], writes=[("adast",)], sem="adal")
        S.op("dve", I("memset", ap=ONES[:, :], constant=1.0), writes=[("ones",)])
        for c in range(8):
            S.dma("sync", I("dma_start", out=X[:, c, :], in_=xT[c * 128:(c + 1) * 128, :]),
                  reads=[], writes=[("X", c, ti) for ti in range(5)], sem=f"x{c}")
        SCC = DER[:, DER_SCC:DER_SCC + 24].rearrange("p (k n) -> p k n", n=3)
        for n, nm in enumerate(("c0", "c1", "cctx")):
            S.op("act", I("activation", out=SCC[:, :, n], in_=vcol(nm, 0, 8), func=AF.Silu),
                 reads=[("vec",)], writes=[("scc", n)])
        fn = []
        for l in range(DEPTH):
            for o in range(3):
                for kc in range(8):
                    fn += I("matmul", out=PS[:, 0, (l * 3 + o) * 3:(l * 3 + o) * 3 + 3],
                            lhsT=ADAST[:, l, kc, o * 128:(o + 1) * 128],
                            rhs=DER[:, DER_SCC + kc * 3:DER_SCC + kc * 3 + 3], start=(kc == 0), stop=(kc == 7))
        S.op("pe", fn, reads=[("adast",)] + [("scc", n) for n in range(3)], writes=[("ps", 0)])
        S.op("dve", I("tensor_copy", out=AD18, in_=PS[:, 0, 0:18]), reads=[("ps", 0)], writes=[("ad18",)])
        S.dma("pool", I("dma_start", out=cc_ada_in.ap(), in_=AD18), reads=[("ad18",)], writes=[("ccadain",)], sem="pk0")
        S.dma("pool", I("collective_compute", kind="AllGather", op=ALU.bypass, replica_groups=[list(range(NCORES))],
                        ins=[cc_ada_in.ap().opt()], outs=[cc_ada_out.ap().opt()]),
              reads=[("ccadain",)], writes=[("ccadaout",)], sem="cc", inc=1)
        S.dma("pool", I("dma_start", out=ADG[:, :].rearrange("p (r f) -> p r f", r=NCORES),
                        in_=cc_ada_out.ap().rearrange("(r p) f -> p r f", p=128)),
              reads=[("ccadaout",)], writes=[("adg",)], sem="pk0")
        GV = ADG[:, :].rearrange("p (r l o n) -> p l n r o", r=NCORES, l=DEPTH, o=3, n=3)
        T24v = T24.rearrange("p (r o) -> p r o", o=3)
        for l in range(DEPTH):
            base = l * LD
            bada3 = vcol(("bada", l), 0, 24).rearrange("p (r o) -> p r o", o=3)
            S.op("dve", I("tensor_scalar", out=T24v, in0=GV[:, l, 0], scalar1=vcol("sel0"), scalar2=None, op0=ALU.mult),
                 reads=[("adg",), ("vec",), ("t24",)], writes=[("t24",)])
            S.op("dve", I("scalar_tensor_tensor", out=T24v, in0=GV[:, l, 1], scalar=vcol("sel1"), in1=T24v,
                          op0=ALU.mult, op1=ALU.add), reads=[("adg",), ("t24",)], writes=[("t24",)])
            S.op("dve", I("tensor_tensor", out=DER[:, base:base + 24], in0=T24, in1=vcol(("bada", l), 0, 24), op=ALU.add),
                 reads=[("t24",)], writes=[("mod", l, 0)])
            S.op("dve", I("tensor_tensor", out=DER[:, base + 24:base + 48].rearrange("p (r o) -> p r o", o=3),
                          in0=GV[:, l, 2], in1=bada3, op=ALU.add), reads=[("adg",), ("vec",)], writes=[("mod", l, 1)])
            for n in range(2):
                mb = base + n * 24
                S.op("dve", I("tensor_scalar", out=T8, in0=DER[:, mb + 8:mb + 16], scalar1=1.0, scalar2=None, op0=ALU.add),
                     reads=[("mod", l, n), ("t8",)], writes=[("t8",)])
                S.op("dve", I("tensor_tensor", out=DER[:, base + 48 + n * 8:base + 56 + n * 8], in0=T8,
                              in1=vcol(("g", l), 0, 8), op=ALU.mult), reads=[("t8",), ("vec",)], writes=[("gs", l, n)])
            S.op("act", I("activation", out=T16A, in_=vcol(("lam", l), 0, 16), func=AF.Exp, scale=-1.0),
                 reads=[("vec",), ("t16a",)], writes=[("t16a",)])
            S.op("dve", I("tensor_scalar", out=T16B, in0=T16A, scalar1=-0.25, scalar2=1.0 / 3.0, op0=ALU.mult, op1=ALU.add),
                 reads=[("t16a",), ("t16b",)], writes=[("t16b",)])
            for cst in (-0.5, 1.0):
                S.op("dve", I("tensor_tensor", out=T16B, in0=T16B, in1=T16A, op=ALU.mult),
                     reads=[("t16a",), ("t16b",)], writes=[("t16b",)])
                S.op("dve", I("tensor_scalar", out=T16B, in0=T16B, scalar1=cst, scalar2=None, op0=ALU.add),
                     reads=[("t16b",)], writes=[("t16b",)])
            S.op("dve", I("tensor_tensor", out=T16B, in0=T16B, in1=T16A, op=ALU.mult),
                 reads=[("t16a",), ("t16b",)], writes=[("t16b",)])
            S.op("dve", I("tensor_scalar", out=DER[:, base + 64:base + 80], in0=T16B, scalar1=-4.0, scalar2=None, op0=ALU.mult),
                 reads=[("t16b",)], writes=[("hc1", l)])
            S.op("dve", I("tensor_scalar", out=DER[:, base + 80:base + 96], in0=T16B, scalar1=-4.0 * LAT, scalar2=None,
                          op0=ALU.mult), reads=[("t16b",)], writes=[("hc1n", l)])
            S.op("dve", I("tensor_scalar", out=DER[:, base + 96:base + 112], in0=vcol(("br", l), 0, 16), scalar1=0.5,
                          scalar2=None, op0=ALU.mult), reads=[("vec",)], writes=[("hbr", l)])
            S.op("dve", I("tensor_scalar", out=DER[:, base + 112:base + 128], in0=vcol(("bi", l), 0, 16), scalar1=0.5,
                          scalar2=None, op0=ALU.mult), reads=[("vec",)], writes=[("hbi", l)])
        S.barrier()

        def stage_norm(l, final):
            for ti, (t0, N, RL) in enumerate(TILES):
                if final and ti == 0:
                    continue
                n = 1 if ti == 0 else 0
                bank = psalloc()
                for kc in range(8):
                    s = talloc()
                    S.op("act", (I("activation",
                        out=tap(s, N), in_=X[:, kc, t0:t0 + N], func=AF.Square)),
                        reads=[("X", kc, ti)], writes=[("t", s)])
                    S.op("pe", (I("matmul",
                        out=PS[:, bank, 0:N], lhsT=ONES[:, :], rhs=tap(s, N), start=(kc == 0), stop=(kc == 7))),
                        reads=[("t", s), ("ones",)], writes=[("ps", bank)])
                sd = talloc()
                S.op("act", (I("activation",
                    out=tap(sd, N), in_=PS[:, bank, 0:N], func=AF.Sqrt, scale=1.0 / D, bias=vcol("eps"))),
                    reads=[("ps", bank)], writes=[("t", sd)])
                S.op("dve", (I("reciprocal", out=tap(sd, N), in_=tap(sd, N))),
                     reads=[("t", sd)], writes=[("t", sd)])
                for kc in range(8):
                    if final:
                        S.op("dve", (I("scalar_tensor_tensor",
                            out=X[:, kc, t0:t0 + N], in0=X[:, kc, t0:t0 + N], scalar=vcol("fg", kc),
                            in1=tap(sd, N), op0=ALU.mult, op1=ALU.mult)),
                            reads=[("X", kc, ti), ("t", sd)], writes=[("X", kc, ti)])
                    else:
                        s = talloc()
                        S.op("pool", (I("tensor_tensor",
                            out=tap(s, N), in0=X[:, kc, t0:t0 + N], in1=tap(sd, N), op=ALU.mult)),
                            reads=[("X", kc, ti), ("t", sd)], writes=[("t", s)])
                        S.op("act", (I("activation",
                            out=H[:, kc, t0:t0 + N], in_=tap(s, N), func=AF.Identity,
                            scale=gs_col(l, kc, n), bias=mod_col(l, 0, kc, n))),
                            reads=[("t", s)], writes=[("h", kc, ti)])

        def hkeys(ti):
            return [("h", kc, ti) for kc in range(8)]

        def stage_lru(l, last):
            lat = [1, 2, 3, 4]
            for j in range(9):
                if j < 8:
                    st_ = LSETS[j % 2]
                    Ad, Ud = st_["A"], st_["U"]
                    pj = j % 2
                    s4 = wacquire(("in", l, 4, j))
                    sg = wacquire(("gate", l, j))
                    w4 = lambda t: vcol(("w4", l), t * 8 + j)
                    bxs = []
                    for ti, (t0, N, RL) in enumerate(TILES):
                        bx = psalloc()
                        bxs.append(bx)
                        mm8(bx, N, s4, (lambda kc, t0=t0, N=N: H[:, kc, t0:t0 + N]), hkeys(ti))
                    for ti, (t0, N, RL) in enumerate(TILES):
                        S.op("act", I("activation", out=XC[:, t0:t0 + N], in_=PS[:, bxs[ti], 0:N], func=AF.Identity,
                                      scale=w4(2), bias=vcol(("b4", l), j)),
                             reads=[("ps", bxs[ti])], writes=[("XC", ti)])
                    for tapi, sh_o, sh_i in ((0, 2, 0), (1, 1, 0), (3, 0, 1)):
                        for ti, (t0, N, RL) in enumerate(TILES):
                            xc3 = XC[:, t0:t0 + N].rearrange("p (r w) -> p r w", w=RL)
                            ps3 = PS[:, bxs[ti], 0:N].rearrange("p (r w) -> p r w", w=RL)
                            ln = RL - max(sh_o, sh_i) if tapi != 0 else RL - 2
                            osl = slice(sh_o, sh_o + ln)
                            isl = slice(sh_i, sh_i + ln)
                            S.op("dve", I("scalar_tensor_tensor", out=xc3[:, :, osl], in0=ps3[:, :, isl], scalar=w4(tapi),
                                          in1=xc3[:, :, osl], op0=ALU.mult, op1=ALU.add),
                                 reads=[("ps", bxs[ti]), ("XC", ti)], writes=[("XC", ti)])
                    for ti, (t0, N, RL) in enumerate(TILES):
                        S.op("pool", I("tensor_copy", out=XCB[:, t0:t0 + N], in_=XC[:, t0:t0 + N]),
                             reads=[("XC", ti)], writes=[("xcb", ti)])
                    for d in range(2):
                        for g in range(2):
                            bks = []
                            for ti, (t0, N, RL) in enumerate(TILES):
                                bk = psalloc()
                                bks.append(bk)
                                S.op("pe", I("matmul", out=PS[:, bk, 0:N], lhsT=WS[:, sg, 2 * g + d, :], rhs=XCB[:, t0:t0 + N],
                                             start=True, stop=True),
                                     reads=[("w", sg), ("xcb", ti)], writes=[("ps", bk)])
                            for ti, (t0, N, RL) in enumerate(TILES):
                                hb = dcol(l * LD + (96 if g == 0 else 112) + d * 8 + j)
                                if g == 0:
                                    kw = dict(accum_out=RACC[d][:, ti - 1:ti]) if ti > 0 else {}
                                    S.op("act", I("activation", out=Ad[d][:, t0:t0 + N], in_=PS[:, bks[ti], 0:N], func=AF.Tanh,
                                                  scale=0.5, bias=hb, **kw),
                                         reads=[("ps", bks[ti])], writes=[("A", pj, d, ti)] + ([("racc", d, ti)] if ti > 0 else []))
                                else:
                                    S.op("act", I("activation", out=Ud[d][:, t0:t0 + N], in_=PS[:, bks[ti], 0:N], func=AF.Tanh,
                                                  scale=0.5, bias=hb),
                                         reads=[("ps", bks[ti])], writes=[("U", pj, d, ti)])
                    hc1 = lambda d: dcol(l * LD + 64 + d * 8 + j)
                    hc1n = lambda d: dcol(l * LD + 80 + d * 8 + j)
                    for d in range(2):
                        for ti, (t0, N, RL) in enumerate(TILES):
                            S.op("act", I("activation", out=Ad[d][:, t0:t0 + N], in_=Ad[d][:, t0:t0 + N], func=AF.Exp,
                                          scale=hc1(d), bias=hc1(d)),
                                 reads=[("A", pj, d, ti)], writes=[("A", pj, d, ti)])
                    for d in range(2):
                        for ti, (t0, N, RL) in enumerate(TILES):
                            tm = talloc()
                            S.op("act", I("activation", out=tap(tm, N), in_=Ad[d][:, t0:t0 + N], func=AF.Square),
                                 reads=[("A", pj, d, ti)], writes=[("t", tm)])
                            S.op("act", I("activation", out=tap(tm, N), in_=tap(tm, N), func=AF.Sqrt, scale=-0.25,
                                          bias=vcol("quarter")), reads=[("t", tm)], writes=[("t", tm)])
                            S.op("pool", I("tensor_tensor", out=tap(tm, N), in0=tap(tm, N), in1=XC[:, t0:t0 + N], op=ALU.mult),
                                 reads=[("t", tm), ("XC", ti)], writes=[("t", tm)])
                            S.op("dve", I("scalar_tensor_tensor", out=Ud[d][:, t0:t0 + N], in0=Ud[d][:, t0:t0 + N], scalar=1.0,
                                          in1=tap(tm, N), op0=ALU.add, op1=ALU.mult),
                                 reads=[("U", pj, d, ti), ("t", tm)], writes=[("U", pj, d, ti)])
                    AK = lambda d, tis: [("A", pj, d, ti) for ti in tis]
                    UK = lambda d, tis: [("U", pj, d, ti) for ti in tis]
                    S.op("dve", I("tensor_tensor_scan", out=Ud[0][:, 0:CTX], data0=Ad[0][:, 0:CTX], data1=Ud[0][:, 0:CTX],
                                  initial=vcol("zero"), op0=ALU.mult, op1=ALU.add),
                         reads=AK(0, [0]) + UK(0, [0]), writes=UK(0, [0]))
                    S.op("dve", I("tensor_tensor_scan", out=Ud[1][:, 0:CTX][:, ::-1], data0=Ad[1][:, 0:CTX][:, ::-1],
                                  data1=Ud[1][:, 0:CTX][:, ::-1], initial=vcol("zero"), op0=ALU.mult, op1=ALU.add),
                         reads=AK(1, [0]) + UK(1, [0]), writes=UK(1, [0]))
                    S.op("dve", I("tensor_tensor_scan", out=JUNK[:, :], data0=Ad[0][:, CTX:NT], data1=Ud[0][:, CTX:NT],
                                  initial=vcol("zero"), op0=ALU.mult, op1=ALU.add),
                         reads=AK(0, lat) + UK(0, lat) + [("junk",)], writes=[("junk",)])
                    S.op("dve", I("tensor_copy", out=PKS[pj][:, 0:1], in_=JUNK[:, LAT - 1:LAT]),
                         reads=[("junk",), ("pk", pj)], writes=[("pk", pj)])
                    S.op("dve", I("tensor_tensor_scan", out=JUNK[:, ::-1], data0=Ad[1][:, CTX:NT][:, ::-1],
                                  data1=Ud[1][:, CTX:NT][:, ::-1], initial=vcol("zero"), op0=ALU.mult, op1=ALU.add),
                         reads=AK(1, lat) + UK(1, lat) + [("junk",)], writes=[("junk",)])
                    S.op("dve", I("tensor_copy", out=PKS[pj][:, 1:2], in_=JUNK[:, 0:1]),
                         reads=[("junk",), ("pk", pj)], writes=[("pk", pj)])
                    for d in range(2):
                        S.op("act", I("activation", out=JUNK4, in_=RACC[d], func=AF.Identity, accum_out=RSUM[d]),
                             reads=[("racc", d, ti) for ti in lat] + [("junk4",), ("rsum", d)], writes=[("rsum", d), ("junk4",)])
                        S.op("act", I("activation", out=PKS[pj][:, 2 + d:3 + d], in_=RSUM[d], func=AF.Exp,
                                      scale=hc1(d), bias=hc1n(d)),
                             reads=[("rsum", d), ("pk", pj)], writes=[("pk", pj)])
                    S.dma("pool", I("dma_start", out=cc_in[l][j].ap(), in_=PKS[pj]),
                          reads=[("pk", pj)], writes=[("ccin", l, j)], sem=f"pk{pj}")
                    S.dma("pool", I("collective_compute", kind="AllGather", op=ALU.bypass,
                                    replica_groups=[list(range(NCORES))],
                                    ins=[cc_in[l][j].ap().opt()], outs=[cc_out[l][j].ap().opt()]),
                          reads=[("ccin", l, j)], writes=[("ccout", l, j)], sem="cc", inc=1)
                    S.dma("pool", I("dma_start", out=G3S[pj], in_=cc_out[l][j].ap().rearrange("(r p) f -> p r f", p=128)),
                          reads=[("ccout", l, j)], writes=[("gath", pj)], sem=f"pk{pj}")
                    wrelease(2)
                if j > 0:
                    jj = j - 1
                    pj = jj % 2
                    st_ = LSETS[pj]
                    Ad, Ud = st_["A"], st_["U"]
                    G3 = G3S[pj]
                    AK = lambda d, tis: [("A", pj, d, ti) for ti in tis]
                    UK = lambda d, tis: [("U", pj, d, ti) for ti in tis]
                    s5 = wacquire(("in", l, 5, jj))
                    for d in range(2):
                        msk = vcol("maskf" if d == 0 else "maskb", 0, 8)
                        S.op("dve", I("tensor_scalar", out=PE_[d], in0=G3[:, :, 2 + d], scalar1=-1.0, scalar2=None, op0=ALU.add),
                             reads=[("gath", pj), ("pe", d)], writes=[("pe", d)])
                        S.op("dve", I("tensor_tensor", out=PE_[d], in0=PE_[d], in1=msk, op=ALU.mult),
                             reads=[("pe", d)], writes=[("pe", d)])
                        S.op("dve", I("tensor_scalar", out=PE_[d], in0=PE_[d], scalar1=1.0, scalar2=None, op0=ALU.add),
                             reads=[("pe", d)], writes=[("pe", d)])
                        S.op("dve", I("tensor_tensor", out=FE_[d], in0=G3[:, :, d], in1=msk, op=ALU.mult),
                             reads=[("gath", pj), ("fe", d)], writes=[("fe", d)])
                    S.op("dve", I("tensor_tensor_scan", out=FOLD[0], data0=PE_[0], data1=FE_[0],
                                  initial=Ud[0][:, CTX - 1:CTX], op0=ALU.mult, op1=ALU.add),
                         reads=[("pe", 0), ("fe", 0), ("fold", 0)] + UK(0, [0]), writes=[("fold", 0)])
                    S.op("dve", I("tensor_tensor_scan", out=FOLD[1][:, ::-1], data0=PE_[1][:, ::-1], data1=FE_[1][:, ::-1],
                                  initial=Ud[1][:, 0:1], op0=ALU.mult, op1=ALU.add),
                         reads=[("pe", 1), ("fe", 1), ("fold", 1)] + UK(1, [0]), writes=[("fold", 1)])
                    S.op("dve", I("tensor_tensor_scan", out=Ud[0][:, CTX:NT], data0=Ad[0][:, CTX:NT], data1=Ud[0][:, CTX:NT],
                                  initial=FOLD[0][:, 7:8], op0=ALU.mult, op1=ALU.add),
                         reads=AK(0, lat) + UK(0, lat) + [("fold", 0)], writes=UK(0, lat))
                    S.op("dve", I("tensor_tensor_scan", out=Ud[1][:, CTX:NT][:, ::-1], data0=Ad[1][:, CTX:NT][:, ::-1],
                                  data1=Ud[1][:, CTX:NT][:, ::-1], initial=FOLD[1][:, 0:1], op0=ALU.mult, op1=ALU.add),
                         reads=AK(1, lat) + UK(1, lat) + [("fold", 1)], writes=UK(1, lat))
                    tl = [(ti, t) for ti, t in enumerate(TILES) if not (last and ti == 0)]
                    bzs, ths, sss = {}, {}, {}
                    for ti, (t0, N, RL) in tl:
                        bzs[ti] = psalloc()
                        mm8(bzs[ti], N, s5, (lambda kc, t0=t0, N=N: H[:, kc, t0:t0 + N]), hkeys(ti))
                    for ti, (t0, N, RL) in tl:
                        ths[ti] = talloc()
                        S.op("act", I("activation", out=tap(ths[ti], N), in_=PS[:, bzs[ti], 0:N], func=AF.Tanh, scale=0.5),
                             reads=[("ps", bzs[ti])], writes=[("t", ths[ti])])
                    for ti, (t0, N, RL) in tl:
                        S.op("dve", I("scalar_tensor_tensor", out=tap(ths[ti], N), in0=tap(ths[ti], N), scalar=1.0,
                                      in1=PS[:, bzs[ti], 0:N], op0=ALU.add, op1=ALU.mult),
                             reads=[("t", ths[ti]), ("ps", bzs[ti])], writes=[("t", ths[ti])])
                    for ti, (t0, N, RL) in tl:
                        sss[ti] = talloc()
                        S.op("pool", I("tensor_tensor", out=tap(sss[ti], N), in0=Ud[0][:, t0:t0 + N], in1=Ud[1][:, t0:t0 + N],
                                       op=ALU.add),
                             reads=[("U", pj, 0, ti), ("U", pj, 1, ti)], writes=[("t", sss[ti])])
                    for ti, (t0, N, RL) in tl:
                        S.op("dve", I("scalar_tensor_tensor", out=YB[:, jj, t0:t0 + N], in0=tap(ths[ti], N), scalar=0.5,
                                      in1=tap(sss[ti], N), op0=ALU.mult, op1=ALU.mult),
                             reads=[("t", ths[ti]), ("t", sss[ti])], writes=[("yb", jj, ti)])
                    wrelease(1)

        def stage_conv(l, last):
            for j in range(8):
                sv = wacquire(("in", l, 0, j))
                sc = wacquire(("in", l, 2, j))
                sb = wacquire(("in", l, 1, j))
                sz_ = wacquire(("in", l, 3, j))
                w3 = lambda t: vcol(("w3", l), t * 8 + j)
                for ti, (t0, N, RL) in enumerate(TILES):
                    if last and ti == 0:
                        continue
                    rhs = (lambda kc, t0=t0, N=N: H[:, kc, t0:t0 + N])
                    bv, bc, bb, bz = psalloc(), psalloc(), psalloc(), psalloc()
                    mm8(bv, N, sv, rhs, hkeys(ti))
                    mm8(bc, N, sc, rhs, hkeys(ti))
                    mm8(bb, N, sb, rhs, hkeys(ti))
                    mm8(bz, N, sz_, rhs, hkeys(ti))
                    tv, tg, tc, ts = talloc(), talloc(), talloc(), talloc()
                    S.op("act", (I("activation", out=tap(tv, N), in_=PS[:, bv, 0:N], func=AF.Copy)),
                         reads=[("ps", bv)], writes=[("t", tv)])
                    S.op("dve", (I("tensor_tensor",
                        out=tap(tg, N), in0=PS[:, bc, 0:N], in1=tap(tv, N), op=ALU.mult)),
                        reads=[("ps", bc), ("t", tv)], writes=[("t", tg)])
                    S.op("pool", (I("tensor_scalar",
                        out=tap(tc, N), in0=tap(tg, N), scalar1=w3(1), scalar2=None, op0=ALU.mult)),
                        reads=[("t", tg)], writes=[("t", tc)])
                    g3 = tap(tg, N).rearrange("p (r w) -> p r w", w=RL)
                    c3 = tap(tc, N).rearrange("p (r w) -> p r w", w=RL)
                    for tapi, (osl, isl) in ((0, (slice(1, RL), slice(0, RL - 1))),
                                             (2, (slice(0, RL - 1), slice(1, RL)))):
                        S.op("dve", (I("scalar_tensor_tensor",
                            out=c3[:, :, osl], in0=g3[:, :, isl], scalar=w3(tapi), in1=c3[:, :, osl],
                            op0=ALU.mult, op1=ALU.add)),
                            reads=[("t", tg), ("t", tc)], writes=[("t", tc)])
                    S.op("act", (I("activation", out=tap(ts, N), in_=PS[:, bz, 0:N], func=AF.Silu)),
                         reads=[("ps", bz)], writes=[("t", ts)])
                    S.op("dve", (I("tensor_tensor",
                        out=tap(tc, N), in0=PS[:, bb, 0:N], in1=tap(tc, N), op=ALU.mult)),
                        reads=[("ps", bb), ("t", tc)], writes=[("t", tc)])
                    S.op("pool", (I("tensor_tensor",
                        out=YA[:, j, t0:t0 + N], in0=tap(tc, N), in1=tap(ts, N), op=ALU.mult)),
                        reads=[("t", tc), ("t", ts)], writes=[("ya", j, ti)])
                wrelease(4)

        def stage_merge(l, last):
            for jo in range(8):
                s6 = wacquire(("in", l, 6, jo))
                s7 = wacquire(("in", l, 7, jo))
                sa = wacquire(("oa", l, jo))
                sb = wacquire(("ob", l, jo))
                for ti, (t0, N, RL) in enumerate(TILES):
                    if last and ti == 0:
                        continue
                    bma, bmb, bpa, bpb = psalloc(), psalloc(), psalloc(), psalloc()
                    mm8(bma, N, s6, (lambda kc, t0=t0, N=N: H[:, kc, t0:t0 + N]), hkeys(ti))
                    mm8(bmb, N, s7, (lambda kc, t0=t0, N=N: H[:, kc, t0:t0 + N]), hkeys(ti))
                    mm8(bpa, N, sa, (lambda kc, t0=t0, N=N: YA[:, kc, t0:t0 + N]), [("ya", kc, ti) for kc in range(8)])
                    mm8(bpb, N, sb, (lambda kc, t0=t0, N=N: YB[:, kc, t0:t0 + N]), [("yb", kc, ti) for kc in range(8)])
                    ta, tb = talloc(), talloc()
                    S.op("act", (I("activation", out=tap(ta, N), in_=PS[:, bma, 0:N], func=AF.Sigmoid)),
                         reads=[("ps", bma)], writes=[("t", ta)])
                    S.op("act", (I("activation", out=tap(tb, N), in_=PS[:, bmb, 0:N], func=AF.Sigmoid)),
                         reads=[("ps", bmb)], writes=[("t", tb)])
                    S.op("dve", (I("tensor_tensor",
                        out=tap(ta, N), in0=PS[:, bpa, 0:N], in1=tap(ta, N), op=ALU.mult)),
                        reads=[("ps", bpa), ("t", ta)], writes=[("t", ta)])
                    S.op("dve", (I("tensor_tensor",
                        out=tap(tb, N), in0=PS[:, bpb, 0:N], in1=tap(tb, N), op=ALU.mult)),
                        reads=[("ps", bpb), ("t", tb)], writes=[("t", tb)])
                    S.op("pool", (I("tensor_tensor",
                        out=MG[:, jo, t0:t0 + N], in0=tap(ta, N), in1=tap(tb, N), op=ALU.add)),
                        reads=[("t", ta), ("t", tb)], writes=[("mg", jo, ti)])
                wrelease(4)

        def stage_out(l, last):
            src = xT if l == 0 else xs
            for jo in range(8):
                so = wacquire(("o", l, jo))
                S.dma("sync", (I("dma_start", out=X[:, jo, :], in_=src[jo * 128:(jo + 1) * 128, :])),
                      reads=[("xs", jo)], writes=[("X", jo, ti) for ti in range(5)], sem=f"x{jo}")
                for ti, (t0, N, RL) in enumerate(TILES):
                    if last and ti == 0:
                        continue
                    n = 1 if ti == 0 else 0
                    bo = psalloc()
                    mm8(bo, N, so, (lambda kc, t0=t0, N=N: MG[:, kc, t0:t0 + N]), [("mg", kc, ti) for kc in range(8)])
                    S.op("dve", (I("scalar_tensor_tensor",
                        out=X[:, jo, t0:t0 + N], in0=PS[:, bo, 0:N], scalar=mod_col(l, 2, jo, n),
                        in1=X[:, jo, t0:t0 + N], op0=ALU.mult, op1=ALU.add)),
                        reads=[("ps", bo), ("X", jo, ti)], writes=[("X", jo, ti)])
                if not last:
                    S.dma("sync", (I("dma_start", out=xs[jo * 128:(jo + 1) * 128, :], in_=X[:, jo, :])),
                          reads=[("X", jo, ti) for ti in range(5)], writes=[("xs", jo)], sem=f"x{jo}")
                wrelease(1)

        for l in range(depth):
            last = (l == DEPTH - 1)
            stage_norm(l, False)
            S.barrier()
            stage_lru(l, last)
            S.barrier()
            stage_conv(l, last)
            stage_merge(l, last)
            S.barrier()
            stage_out(l, last)
        if final_norm:
            stage_norm(None, True)
        for kc in range(8):
            S.dma("sync", (I("dma_start", out=outT[kc * 128:(kc + 1) * 128, :], in_=X[:, kc, CTX:NT])),
                  reads=[("X", kc, ti) for ti in range(1, 5)], writes=[("out", kc)], sem="out")
        S.barrier()

        sem_names = ["pe", "act", "dve", "pool"] + sorted(S.dcum.keys())
        sems = {n: es.enter_context(nc.semaphore("s_" + n)) for n in sem_names}
        block = es.enter_context(nc.Block())

        def emit(eng_name):
            def body(e):
                for waits, fn, inc in S.q[eng_name]:
                    for s, v in waits:
                        e.wait_ge(sems[s], v)
                    if fn is None:
                        continue
                    ins = None
                    for m, kw in fn:
                        ins = getattr(e, m)(**kw)
                    ins.then_inc(sems[inc[0]], inc[1])
            return body

        block.tensor(emit("pe"))
        block.scalar(emit("act"))
        block.vector(emit("dve"))
        block.gpsimd(emit("pool"))
        block.sync(emit("sync"))
    return nc


_NC_CACHE = {}


def _prep_inputs(inputs):
    f = lambda k: np.asarray(inputs[k], dtype=np.float32)
    x, c, ctx, c_ctx = f("x"), f("c"), f("ctx"), f("c_ctx")
    shared = {k: np.ascontiguousarray(f(k)) for k in ("w_in", "lru_wr", "lru_wi", "w_out_a", "w_out_b", "w_o")}
    per_layer = []
    for l in range(DEPTH):
        per_layer += [_lay(f("norm_g")[l]), _lay(f("b_ada")[l].reshape(3, D)), _lay(f("conv3_w")[l]),
                      _lay(f("conv4_w")[l]), _lay(f("conv4_b")[l]), _lay(f("lru_br")[l]), _lay(f("lru_bi")[l]),
                      _lay(f("lru_lambda")[l])]
    in_maps = []
    for r in range(NCORES):
        b, q = divmod(r, QPB)
        xt = np.concatenate([ctx[b], x[b, q * LAT:(q + 1) * LAT]], axis=0)
        xT = np.ascontiguousarray(xt.T)
        maskf = np.zeros((128, NCORES), np.float32)
        maskb = np.zeros((128, NCORES), np.float32)
        for rr in range(NCORES):
            if rr // QPB == b and rr < r:
                maskf[:, rr] = 1.0
            if rr // QPB == b and rr > r:
                maskb[:, rr] = 1.0
        consts = np.zeros((128, 6), np.float32)
        consts[:, 0] = 1.0
        consts[:, 1] = RMS_EPS
        consts[:, 3] = 0.25
        consts[:, 4 + b] = 1.0
        vec = np.concatenate(per_layer + [_lay(f("final_g")), _lay(c[0]), _lay(c[1]), _lay(c_ctx), maskf, maskb, consts],
                             axis=1)
        assert vec.shape == (128, NV), vec.shape
        m = {"xT": xT, "vec": np.ascontiguousarray(vec),
             "w_ada_s": np.ascontiguousarray(f("w_ada")[:, :, r * 384:(r + 1) * 384])}
        m.update(shared)
        in_maps.append(m)
    return in_maps


def kernel(**inputs):
    if "nc" not in _NC_CACHE:
        _NC_CACHE["nc"] = build_nc()
    nc = _NC_CACHE["nc"]
    in_maps = _prep_inputs(inputs)
    res = run_bass_kernel_spmd(nc, in_maps, core_ids=list(range(NCORES)))
    out = np.empty((NB, SEQ, D), np.float32)
    for r in range(NCORES):
        b, q = divmod(r, QPB)
        out[b, q * LAT:(q + 1) * LAT, :] = np.asarray(res.results[r]["outT"]).T
    return out
```

```python
import contextlib
import numpy as np
import concourse.bass as bass
import concourse.mybir as mybir
from concourse.bass_utils import run_bass_kernel_spmd

F32 = mybir.dt.float32
BF16 = mybir.dt.bfloat16
AF = mybir.ActivationFunctionType
ALU = mybir.AluOpType

D = 1024
NB = 2
SEQ = 8192
DEPTH = 2
CTX = 256
GW = 64
NCORES = 8
QPB = 4
LAT = SEQ // QPB
NT = CTX + LAT
KC = 8
RMS_EPS = 1e-6
TILES = [(0, 256, 256)] + [(CTX + 512 * i, 512, GW) for i in range(4)]
NSLOT = 8
NTEMP = 10

_VL = {}
_off = 0
for _l in range(DEPTH):
    for _name, _n in (("g", 8), ("bada", 24), ("w3", 24), ("w4", 32), ("b4", 8), ("br", 16), ("bi", 16), ("lam", 16)):
        _VL[(_name, _l)] = _off
        _off += _n
for _name, _n in (("fg", 8), ("c0", 8), ("c1", 8), ("cctx", 8), ("maskf", 8), ("maskb", 8), ("one", 1), ("eps", 1),
                  ("zero", 1), ("quarter", 1), ("sel0", 1), ("sel1", 1)):
    _VL[_name] = _off
    _off += _n
NV = _off

LD = 128
DER_SCC = DEPTH * LD
NDER = DER_SCC + 24


def _lay(v):
    v = np.asarray(v, dtype=np.float32).reshape(-1, 8, 128)
    return np.ascontiguousarray(v.transpose(2, 0, 1).reshape(128, -1))


def I(method, **kw):
    return [(method, kw)]


class Sched:
    ENGS = ("pe", "act", "dve", "pool", "sync")

    def __init__(self):
        self.q = {e: [] for e in self.ENGS}
        self.cnt = {e: 0 for e in self.ENGS}
        self.lw = {}
        self.rd = {}
        self.seen = {e: {} for e in self.ENGS}
        self.dcum = {}
        self.same_engine_sync = True

    def _waits(self, eng, reads, writes):
        toks = []
        for k in reads:
            t = self.lw.get(k)
            if t is not None:
                toks.append(t)
        for k in writes:
            t = self.lw.get(k)
            if t is not None:
                toks.append(t)
            toks.extend(self.rd.get(k, ()))
        need = {}
        for s, v in toks:
            if s == eng and (eng == "pe" or not self.same_engine_sync):
                continue
            if need.get(s, 0) < v:
                need[s] = v
        out = []
        for s, v in need.items():
            if self.seen[eng].get(s, 0) >= v:
                continue
            self.seen[eng][s] = v
            out.append((s, v))
        return out

    def _commit(self, tok, reads, writes):
        for k in writes:
            self.lw[k] = tok
            self.rd[k] = []
        for k in reads:
            if k not in writes:
                self.rd.setdefault(k, []).append(tok)

    def op(self, eng, fn, reads=(), writes=()):
        reads, writes = list(reads), list(writes)
        w = self._waits(eng, reads, writes)
        self.cnt[eng] += 1
        tok = (eng, self.cnt[eng])
        self.q[eng].append((w, fn, (eng, 1)))
        self._commit(tok, reads, writes)
        return tok

    def dma(self, queue, fn, reads, writes, sem, n=1, inc=16):
        reads, writes = list(reads), list(writes)
        w = self._waits(queue, reads, writes)
        self.dcum[sem] = self.dcum.get(sem, 0) + inc * n
        tok = (sem, self.dcum[sem])
        self.q[queue].append((w, fn, (sem, inc)))
        self._commit(tok, reads, writes)
        return tok

    def barrier(self):
        toks = [(e, c) for e, c in self.cnt.items() if c > 0] + list(self.dcum.items())
        for eng in self.ENGS:
            w = []
            for s, v in toks:
                if s == eng:
                    continue
                if self.seen[eng].get(s, 0) >= v:
                    continue
                self.seen[eng][s] = v
                w.append((s, v))
            if w:
                self.q[eng].append((w, None, None))
        self.lw.clear()
        self.rd.clear()


def build_nc(depth=DEPTH, final_norm=True):
    nc = bass.Bass("TRN2", target_bir_lowering=False)
    S = Sched()

    xT = nc.dram_tensor("xT", [D, NT], F32, kind="ExternalInput").ap()
    vec_d = nc.dram_tensor("vec", [128, NV], F32, kind="ExternalInput").ap()
    w_ada_s = nc.dram_tensor("w_ada_s", [DEPTH, D, 384], F32, kind="ExternalInput").ap()
    w_in_d = nc.dram_tensor("w_in", [DEPTH, D, 8 * D], F32, kind="ExternalInput").ap()
    wr_d = nc.dram_tensor("lru_wr", [DEPTH, 2, 8, 128, 128], F32, kind="ExternalInput").ap()
    wi_d = nc.dram_tensor("lru_wi", [DEPTH, 2, 8, 128, 128], F32, kind="ExternalInput").ap()
    woa_d = nc.dram_tensor("w_out_a", [DEPTH, D, D], F32, kind="ExternalInput").ap()
    wob_d = nc.dram_tensor("w_out_b", [DEPTH, D, D], F32, kind="ExternalInput").ap()
    wo_d = nc.dram_tensor("w_o", [DEPTH, D, D], F32, kind="ExternalInput").ap()
    outT = nc.dram_tensor("outT", [D, LAT], F32, kind="ExternalOutput").ap()
    xs = nc.dram_tensor("xs", [D, NT], F32).ap()
    cc_ada_in = nc.dram_tensor("cc_ada_in", [128, 18], F32)
    cc_ada_out = nc.dram_tensor("cc_ada_out", [NCORES * 128, 18], F32)
    cc_in = [[nc.dram_tensor(f"ccin_{l}_{j}", [128, 4], F32) for j in range(8)] for l in range(DEPTH)]
    cc_out = [[nc.dram_tensor(f"ccout_{l}_{j}", [NCORES * 128, 4], F32) for j in range(8)] for l in range(DEPTH)]

    es = contextlib.ExitStack()
    with es:
        REG_H = es.enter_context(nc.sbuf_tensor("reg_h", [128, 8 * NT], BF16))
        REG_M = es.enter_context(nc.sbuf_tensor("reg_m", [128, 8 * NT], BF16))
        BIG = es.enter_context(nc.sbuf_tensor("reg_big", [128, 21888], F32))
        WR = es.enter_context(nc.sbuf_tensor("wring", [128, NSLOT * 1024], BF16))
        TMP = es.enter_context(nc.sbuf_tensor("tmps", [128, NTEMP * 512], F32))
        VEC = es.enter_context(nc.sbuf_tensor("vecs", [128, NV], F32))
        DER = es.enter_context(nc.sbuf_tensor("der", [128, NDER], F32))
        SM = es.enter_context(nc.sbuf_tensor("sm", [128, 256], F32))
        ADG = es.enter_context(nc.sbuf_tensor("adg", [128, NCORES * 18], F32))
        JUNK = es.enter_context(nc.sbuf_tensor("junk", [128, LAT], F32))
        ONES = es.enter_context(nc.sbuf_tensor("ones", [128, 128], F32))
        PS = es.enter_context(nc.psum_tensor("ps", [128, 8, 512], F32))

        H = REG_H[:, :].rearrange("p (k t) -> p k t", k=8)
        MG = REG_M[:, :].rearrange("p (k t) -> p k t", k=8)
        MF = REG_M[:, :].bitcast(F32)
        ADAST = MF[:, 0:6144].rearrange("p (l k n) -> p l k n", l=DEPTH, k=8)
        X = BIG[:, 0:8 * NT].rearrange("p (k t) -> p k t", k=8)
        YB = BIG[:, 0:9216].bitcast(BF16).rearrange("p (k t) -> p k t", k=8)
        YA = BIG[:, 9216:18432].bitcast(BF16).rearrange("p (k t) -> p k t", k=8)
        A_ = [BIG[:, 9216:11520], BIG[:, 11520:13824]]
        U_ = [BIG[:, 13824:16128], BIG[:, 16128:18432]]
        XC = BIG[:, 18432:20736]
        XCB = BIG[:, 20736:21888].bitcast(BF16)
        WS = WR[:, :].rearrange("p (s k m) -> p s k m", s=NSLOT, k=8)

        def vcol(name, i=0, n=1):
            o = _VL[name] + i
            return VEC[:, o:o + n]

        def dcol(o, n=1):
            return DER[:, o:o + n]

        def mod_col(l, which, kc, n):
            return dcol(l * LD + n * 24 + which * 8 + kc)

        def gs_col(l, kc, n):
            return dcol(l * LD + 48 + n * 8 + kc)

        def c1_col(l, d, j):
            return dcol(l * LD + 64 + d * 8 + j)

        def c2_col(l, d, j):
            return dcol(l * LD + 80 + d * 8 + j)

        PKS = [SM[:, 0:4], SM[:, 202:206]]
        G3S = [SM[:, 4:36].rearrange("p (r f) -> p r f", f=4), SM[:, 206:238].rearrange("p (r f) -> p r f", f=4)]
        AD18 = SM[:, 160:178]
        T24 = SM[:, 178:202]
        LSETS = [dict(A=[BIG[:, 9216:11520], BIG[:, 11520:13824]], U=[BIG[:, 13824:16128], BIG[:, 16128:18432]]),
                 dict(A=[MF[:, 0:2304], MF[:, 2304:4608]], U=[MF[:, 4608:6912], MF[:, 6912:9216]])]
        PE_ = [SM[:, 36:44], SM[:, 52:60]]
        FE_ = [SM[:, 44:52], SM[:, 60:68]]
        FOLD = [SM[:, 68:76], SM[:, 76:84]]
        RACC = [SM[:, 84:88], SM[:, 88:92]]
        RSUM = [SM[:, 92:93], SM[:, 93:94]]
        JUNK4 = SM[:, 94:98]
        T8 = SM[:, 102:110]
        T16A = SM[:, 110:126]
        T16B = SM[:, 126:142]

        st = {"ps": 0, "t": 0}

        def psalloc():
            b = st["ps"]
            st["ps"] = (b + 1) % 8
            return b

        def talloc():
            s = st["t"]
            st["t"] = (s + 1) % NTEMP
            return s

        def tap(s, N):
            return TMP[:, s * 512:s * 512 + N]

        wplan = []
        for l in range(depth):
            for j in range(9):
                if j < 8:
                    wplan += [("in", l, 4, j), ("gate", l, j)]
                if j > 0:
                    wplan += [("in", l, 5, j - 1)]
            for j in range(8):
                wplan += [("in", l, 0, j), ("in", l, 2, j), ("in", l, 1, j), ("in", l, 3, j)]
            for j in range(8):
                wplan += [("in", l, 6, j), ("in", l, 7, j), ("oa", l, j), ("ob", l, j)]
            for j in range(8):
                wplan += [("o", l, j)]
        wst = {"issued": 0, "consumed": 0, "released": 0}

        def issue_wload(idx):
            spec = wplan[idx]
            slot = idx % NSLOT
            kind = spec[0]
            if kind == "gate":
                _, l, j = spec
                srcs = [(WS[:, slot, 0:2, :], wr_d[l, :, j].rearrange("d k m -> k d m")),
                        (WS[:, slot, 2:4, :], wi_d[l, :, j].rearrange("d k m -> k d m"))]
            else:
                if kind == "in":
                    _, l, c, j = spec
                    src = w_in_d[l].rearrange("(kc p) n -> p kc n", p=128)[:, :, c * D + j * 128:c * D + (j + 1) * 128]
                else:
                    _, l, j = spec
                    base = {"oa": woa_d, "ob": wob_d, "o": wo_d}[kind]
                    src = base[l].rearrange("(kc p) n -> p kc n", p=128)[:, :, j * 128:(j + 1) * 128]
                srcs = [(WS[:, slot, :, :], src)]
            for i, (dst, src) in enumerate(srcs):
                S.dma("pool", (I("dma_start", out=dst, in_=src)),
                      reads=[], writes=[("w", slot)] if i == 0 else [], sem=f"w{slot}")
            if len(srcs) > 1:
                S.lw[("w", slot)] = (f"w{slot}", S.dcum[f"w{slot}"])

        def wpump():
            lim = min(len(wplan), wst["released"] + NSLOT)
            while wst["issued"] < lim:
                issue_wload(wst["issued"])
                wst["issued"] += 1

        def wacquire(spec):
            c = wst["consumed"]
            assert wplan[c] == spec, (wplan[c], spec)
            assert c < wst["released"] + NSLOT
            wpump()
            wst["consumed"] = c + 1
            return c % NSLOT

        def wrelease(n):
            wst["released"] += n
            assert wst["released"] <= wst["consumed"]
            wpump()

        def mm8(bank, N, slot, rhs_fn, rkeys):
            ins = []
            for kc in range(8):
                ins += I("matmul", out=PS[:, bank, 0:N], lhsT=WS[:, slot, kc, :], rhs=rhs_fn(kc),
                         start=(kc == 0), stop=(kc == 7))
            S.op("pe", ins, reads=[("w", slot)] + rkeys, writes=[("ps", bank)])

        S.dma("sync", I("dma_start", out=VEC[:, :], in_=vec_d), reads=[], writes=[("vec",)], sem="vecl")
        S.dma("sync", I("dma_start", out=ADAST, in_=w_ada_s.rearrange("l (kc p) n -> p l kc n", p=128)),
              reads=[], writes=[("adast",)], sem="adal")
        S.op("dve", I("memset", ap=ONES[:, :], constant=1.0), writes=[("ones",)])
        for c in range(8):
            S.dma("sync", I("dma_start", out=X[:, c, :], in_=xT[c * 128:(c + 1) * 128, :]),
                  reads=[], writes=[("X", c, ti) for ti in range(5)], sem=f"x{c}")
        SCC = DER[:, DER_SCC:DER_SCC + 24].rearrange("p (k n) -> p k n", n=3)
        for n, nm in enumerate(("c0", "c1", "cctx")):
            S.op("act", I("activation", out=SCC[:, :, n], in_=vcol(nm, 0, 8), func=AF.Silu),
                 reads=[("vec",)], writes=[("scc", n)])
        fn = []
        for l in range(DEPTH):
            for o in range(3):
                for kc in range(8):
                    fn += I("matmul", out=PS[:, 0, (l * 3 + o) * 3:(l * 3 + o) * 3 + 3],
                            lhsT=ADAST[:, l, kc, o * 128:(o + 1) * 128],
                            rhs=DER[:, DER_SCC + kc * 3:DER_SCC + kc * 3 + 3], start=(kc == 0), stop=(kc == 7))
        S.op("pe", fn, reads=[("adast",)] + [("scc", n) for n in range(3)], writes=[("ps", 0)])
        S.op("dve", I("tensor_copy", out=AD18, in_=PS[:, 0, 0:18]), reads=[("ps", 0)], writes=[("ad18",)])
        S.dma("pool", I("dma_start", out=cc_ada_in.ap(), in_=AD18), reads=[("ad18",)], writes=[("ccadain",)], sem="pk0")
        S.dma("pool", I("collective_compute", kind="AllGather", op=ALU.bypass, replica_groups=[list(range(NCORES))],
                        ins=[cc_ada_in.ap().opt()], outs=[cc_ada_out.ap().opt()]),
              reads=[("ccadain",)], writes=[("ccadaout",)], sem="cc", inc=1)
        S.dma("pool", I("dma_start", out=ADG[:, :].rearrange("p (r f) -> p r f", r=NCORES),
                        in_=cc_ada_out.ap().rearrange("(r p) f -> p r f", p=128)),
              reads=[("ccadaout",)], writes=[("adg",)], sem="pk0")
        GV = ADG[:, :].rearrange("p (r l o n) -> p l n r o", r=NCORES, l=DEPTH, o=3, n=3)
        T24v = T24.rearrange("p (r o) -> p r o", o=3)
        for l in range(DEPTH):
            base = l * LD
            bada3 = vcol(("bada", l), 0, 24).rearrange("p (r o) -> p r o", o=3)
            S.op("dve", I("tensor_scalar", out=T24v, in0=GV[:, l, 0], scalar1=vcol("sel0"), scalar2=None, op0=ALU.mult),
                 reads=[("adg",), ("vec",), ("t24",)], writes=[("t24",)])
            S.op("dve", I("scalar_tensor_tensor", out=T24v, in0=GV[:, l, 1], scalar=vcol("sel1"), in1=T24v,
                          op0=ALU.mult, op1=ALU.add), reads=[("adg",), ("t24",)], writes=[("t24",)])
            S.op("dve", I("tensor_tensor", out=DER[:, base:base + 24], in0=T24, in1=vcol(("bada", l), 0, 24), op=ALU.add),
                 reads=[("t24",)], writes=[("mod", l, 0)])
            S.op("dve", I("tensor_tensor", out=DER[:, base + 24:base + 48].rearrange("p (r o) -> p r o", o=3),
                          in0=GV[:, l, 2], in1=bada3, op=ALU.add), reads=[("adg",), ("vec",)], writes=[("mod", l, 1)])
            for n in range(2):
                mb = base + n * 24
                S.op("dve", I("tensor_scalar", out=T8, in0=DER[:, mb + 8:mb + 16], scalar1=1.0, scalar2=None, op0=ALU.add),
                     reads=[("mod", l, n), ("t8",)], writes=[("t8",)])
                S.op("dve", I("tensor_tensor", out=DER[:, base + 48 + n * 8:base + 56 + n * 8], in0=T8,
                              in1=vcol(("g", l), 0, 8), op=ALU.mult), reads=[("t8",), ("vec",)], writes=[("gs", l, n)])
            S.op("act", I("activation", out=T16A, in_=vcol(("lam", l), 0, 16), func=AF.Exp, scale=-1.0),
                 reads=[("vec",), ("t16a",)], writes=[("t16a",)])
            S.op("dve", I("tensor_scalar", out=T16B, in0=T16A, scalar1=-0.25, scalar2=1.0 / 3.0, op0=ALU.mult, op1=ALU.add),
                 reads=[("t16a",), ("t16b",)], writes=[("t16b",)])
            for cst in (-0.5, 1.0):
                S.op("dve", I("tensor_tensor", out=T16B, in0=T16B, in1=T16A, op=ALU.mult),
                     reads=[("t16a",), ("t16b",)], writes=[("t16b",)])
                S.op("dve", I("tensor_scalar", out=T16B, in0=T16B, scalar1=cst, scalar2=None, op0=ALU.add),
                     reads=[("t16b",)], writes=[("t16b",)])
            S.op("dve", I("tensor_tensor", out=T16B, in0=T16B, in1=T16A, op=ALU.mult),
                 reads=[("t16a",), ("t16b",)], writes=[("t16b",)])
            S.op("dve", I("tensor_scalar", out=DER[:, base + 64:base + 80], in0=T16B, scalar1=-4.0, scalar2=None, op0=ALU.mult),
                 reads=[("t16b",)], writes=[("hc1", l)])
            S.op("dve", I("tensor_scalar", out=DER[:, base + 80:base + 96], in0=T16B, scalar1=-4.0 * LAT, scalar2=None,
                          op0=ALU.mult), reads=[("t16b",)], writes=[("hc1n", l)])
            S.op("dve", I("tensor_scalar", out=DER[:, base + 96:base + 112], in0=vcol(("br", l), 0, 16), scalar1=0.5,
                          scalar2=None, op0=ALU.mult), reads=[("vec",)], writes=[("hbr", l)])
            S.op("dve", I("tensor_scalar", out=DER[:, base + 112:base + 128], in0=vcol(("bi", l), 0, 16), scalar1=0.5,
                          scalar2=None, op0=ALU.mult), reads=[("vec",)], writes=[("hbi", l)])
        S.barrier()

        def stage_norm(l, final):
            for ti, (t0, N, RL) in enumerate(TILES):
                if final and ti == 0:
                    continue
                n = 1 if ti == 0 else 0
                bank = psalloc()
                for kc in range(8):
                    s = talloc()
                    S.op("act", (I("activation",
                        out=tap(s, N), in_=X[:, kc, t0:t0 + N], func=AF.Square)),
                        reads=[("X", kc, ti)], writes=[("t", s)])
                    S.op("pe", (I("matmul",
                        out=PS[:, bank, 0:N], lhsT=ONES[:, :], rhs=tap(s, N), start=(kc == 0), stop=(kc == 7))),
                        reads=[("t", s), ("ones",)], writes=[("ps", bank)])
                sd = talloc()
                S.op("act", (I("activation",
                    out=tap(sd, N), in_=PS[:, bank, 0:N], func=AF.Sqrt, scale=1.0 / D, bias=vcol("eps"))),
                    reads=[("ps", bank)], writes=[("t", sd)])
                S.op("dve", (I("reciprocal", out=tap(sd, N), in_=tap(sd, N))),
                     reads=[("t", sd)], writes=[("t", sd)])
                for kc in range(8):
                    if final:
                        S.op("dve", (I("scalar_tensor_tensor",
                            out=X[:, kc, t0:t0 + N], in0=X[:, kc, t0:t0 + N], scalar=vcol("fg", kc),
                            in1=tap(sd, N), op0=ALU.mult, op1=ALU.mult)),
                            reads=[("X", kc, ti), ("t", sd)], writes=[("X", kc, ti)])
                    else:
                        s = talloc()
                        S.op("dve", (I("scalar_tensor_tensor",
                            out=tap(s, N), in0=X[:, kc, t0:t0 + N], scalar=gs_col(l, kc, n), in1=tap(sd, N),
                            op0=ALU.mult, op1=ALU.mult)),
                            reads=[("X", kc, ti), ("t", sd)], writes=[("t", s)])
                        S.op("act", (I("activation",
                            out=H[:, kc, t0:t0 + N], in_=tap(s, N), func=AF.Identity,
                            bias=mod_col(l, 0, kc, n))),
                            reads=[("t", s)], writes=[("h", kc, ti)])

        def hkeys(ti):
            return [("h", kc, ti) for kc in range(8)]

        def stage_lru(l, last):
            lat = [1, 2, 3, 4]
            for j in range(9):
                if j < 8:
                    st_ = LSETS[j % 2]
                    Ad, Ud = st_["A"], st_["U"]
                    pj = j % 2
                    s4 = wacquire(("in", l, 4, j))
                    sg = wacquire(("gate", l, j))
                    w4 = lambda t: vcol(("w4", l), t * 8 + j)
                    bxs = []
                    for ti, (t0, N, RL) in enumerate(TILES):
                        bx = psalloc()
                        bxs.append(bx)
                        mm8(bx, N, s4, (lambda kc, t0=t0, N=N: H[:, kc, t0:t0 + N]), hkeys(ti))
                    for ti, (t0, N, RL) in enumerate(TILES):
                        S.op("act", I("activation", out=XC[:, t0:t0 + N], in_=PS[:, bxs[ti], 0:N], func=AF.Identity,
                                      scale=w4(2), bias=vcol(("b4", l), j)),
                             reads=[("ps", bxs[ti])], writes=[("XC", ti)])
                    for tapi, sh_o, sh_i in ((0, 2, 0), (1, 1, 0), (3, 0, 1)):
                        for ti, (t0, N, RL) in enumerate(TILES):
                            xc3 = XC[:, t0:t0 + N].rearrange("p (r w) -> p r w", w=RL)
                            ps3 = PS[:, bxs[ti], 0:N].rearrange("p (r w) -> p r w", w=RL)
                            ln = RL - max(sh_o, sh_i) if tapi != 0 else RL - 2
                            osl = slice(sh_o, sh_o + ln)
                            isl = slice(sh_i, sh_i + ln)
                            S.op("dve", I("scalar_tensor_tensor", out=xc3[:, :, osl], in0=ps3[:, :, isl], scalar=w4(tapi),
                                          in1=xc3[:, :, osl], op0=ALU.mult, op1=ALU.add),
                                 reads=[("ps", bxs[ti]), ("XC", ti)], writes=[("XC", ti)])
                    for ti, (t0, N, RL) in enumerate(TILES):
                        S.op("pool", I("tensor_copy", out=XCB[:, t0:t0 + N], in_=XC[:, t0:t0 + N]),
                             reads=[("XC", ti)], writes=[("xcb", ti)])
                    for d in range(2):
                        for g in range(2):
                            bks = []
                            for ti, (t0, N, RL) in enumerate(TILES):
                                bk = psalloc()
                                bks.append(bk)
                                S.op("pe", I("matmul", out=PS[:, bk, 0:N], lhsT=WS[:, sg, 2 * g + d, :], rhs=XCB[:, t0:t0 + N],
                                             start=True, stop=True),
                                     reads=[("w", sg), ("xcb", ti)], writes=[("ps", bk)])
                            for ti, (t0, N, RL) in enumerate(TILES):
                                hb = dcol(l * LD + (96 if g == 0 else 112) + d * 8 + j)
                                if g == 0:
                                    kw = dict(accum_out=RACC[d][:, ti - 1:ti]) if ti > 0 else {}
                                    S.op("act", I("activation", out=Ad[d][:, t0:t0 + N], in_=PS[:, bks[ti], 0:N], func=AF.Tanh,
                                                  scale=0.5, bias=hb, **kw),
                                         reads=[("ps", bks[ti])], writes=[("A", pj, d, ti)] + ([("racc", d, ti)] if ti > 0 else []))
                                else:
                                    S.op("act", I("activation", out=Ud[d][:, t0:t0 + N], in_=PS[:, bks[ti], 0:N], func=AF.Tanh,
                                                  scale=0.5, bias=hb),
                                         reads=[("ps", bks[ti])], writes=[("U", pj, d, ti)])
                    hc1 = lambda d: dcol(l * LD + 64 + d * 8 + j)
                    hc1n = lambda d: dcol(l * LD + 80 + d * 8 + j)
                    for d in range(2):
                        for ti, (t0, N, RL) in enumerate(TILES):
                            S.op("act", I("activation", out=Ad[d][:, t0:t0 + N], in_=Ad[d][:, t0:t0 + N], func=AF.Exp,
                                          scale=hc1(d), bias=hc1(d)),
                                 reads=[("A", pj, d, ti)], writes=[("A", pj, d, ti)])
                    for d in range(2):
                        for ti, (t0, N, RL) in enumerate(TILES):
                            tm = talloc()
                            S.op("act", I("activation", out=tap(tm, N), in_=Ad[d][:, t0:t0 + N], func=AF.Square),
                                 reads=[("A", pj, d, ti)], writes=[("t", tm)])
                            S.op("act", I("activation", out=tap(tm, N), in_=tap(tm, N), func=AF.Sqrt, scale=-0.25,
                                          bias=vcol("quarter")), reads=[("t", tm)], writes=[("t", tm)])
                            S.op("pool", I("tensor_tensor", out=tap(tm, N), in0=tap(tm, N), in1=XC[:, t0:t0 + N], op=ALU.mult),
                                 reads=[("t", tm), ("XC", ti)], writes=[("t", tm)])
                            S.op("dve", I("scalar_tensor_tensor", out=Ud[d][:, t0:t0 + N], in0=Ud[d][:, t0:t0 + N], scalar=1.0,
                                          in1=tap(tm, N), op0=ALU.add, op1=ALU.mult),
                                 reads=[("U", pj, d, ti), ("t", tm)], writes=[("U", pj, d, ti)])
                    AK = lambda d, tis: [("A", pj, d, ti) for ti in tis]
                    UK = lambda d, tis: [("U", pj, d, ti) for ti in tis]
                    S.op("dve", I("tensor_tensor_scan", out=Ud[0][:, 0:CTX], data0=Ad[0][:, 0:CTX], data1=Ud[0][:, 0:CTX],
                                  initial=vcol("zero"), op0=ALU.mult, op1=ALU.add),
                         reads=AK(0, [0]) + UK(0, [0]), writes=UK(0, [0]))
                    S.op("dve", I("tensor_tensor_scan", out=Ud[1][:, 0:CTX][:, ::-1], data0=Ad[1][:, 0:CTX][:, ::-1],
                                  data1=Ud[1][:, 0:CTX][:, ::-1], initial=vcol("zero"), op0=ALU.mult, op1=ALU.add),
                         reads=AK(1, [0]) + UK(1, [0]), writes=UK(1, [0]))
                    S.op("dve", I("tensor_tensor_scan", out=JUNK[:, :], data0=Ad[0][:, CTX:NT], data1=Ud[0][:, CTX:NT],
                                  initial=vcol("zero"), op0=ALU.mult, op1=ALU.add),
                         reads=AK(0, lat) + UK(0, lat) + [("junk",)], writes=[("junk",)])
                    S.op("dve", I("tensor_copy", out=PKS[pj][:, 0:1], in_=JUNK[:, LAT - 1:LAT]),
                         reads=[("junk",), ("pk", pj)], writes=[("pk", pj)])
                    S.op("dve", I("tensor_tensor_scan", out=JUNK[:, ::-1], data0=Ad[1][:, CTX:NT][:, ::-1],
                                  data1=Ud[1][:, CTX:NT][:, ::-1], initial=vcol("zero"), op0=ALU.mult, op1=ALU.add),
                         reads=AK(1, lat) + UK(1, lat) + [("junk",)], writes=[("junk",)])
                    S.op("dve", I("tensor_copy", out=PKS[pj][:, 1:2], in_=JUNK[:, 0:1]),
                         reads=[("junk",), ("pk", pj)], writes=[("pk", pj)])
                    for d in range(2):
                        S.op("act", I("activation", out=JUNK4, in_=RACC[d], func=AF.Identity, accum_out=RSUM[d]),
                             reads=[("racc", d, ti) for ti in lat] + [("junk4",), ("rsum", d)], writes=[("rsum", d), ("junk4",)])
                        S.op("act", I("activation", out=PKS[pj][:, 2 + d:3 + d], in_=RSUM[d], func=AF.Exp,
                                      scale=hc1(d), bias=hc1n(d)),
                             reads=[("rsum", d), ("pk", pj)], writes=[("pk", pj)])
                    S.dma("pool", I("dma_start", out=cc_in[l][j].ap(), in_=PKS[pj]),
                          reads=[("pk", pj)], writes=[("ccin", l, j)], sem=f"pk{pj}")
                    S.dma("pool", I("collective_compute", kind="AllGather", op=ALU.bypass,
                                    replica_groups=[list(range(NCORES))],
                                    ins=[cc_in[l][j].ap().opt()], outs=[cc_out[l][j].ap().opt()]),
                          reads=[("ccin", l, j)], writes=[("ccout", l, j)], sem="cc", inc=1)
                    S.dma("pool", I("dma_start", out=G3S[pj], in_=cc_out[l][j].ap().rearrange("(r p) f -> p r f", p=128)),
                          reads=[("ccout", l, j)], writes=[("gath", pj)], sem=f"pk{pj}")
                    wrelease(2)
                if j > 0:
                    jj = j - 1
                    pj = jj % 2
                    st_ = LSETS[pj]
                    Ad, Ud = st_["A"], st_["U"]
                    G3 = G3S[pj]
                    AK = lambda d, tis: [("A", pj, d, ti) for ti in tis]
                    UK = lambda d, tis: [("U", pj, d, ti) for ti in tis]
                    s5 = wacquire(("in", l, 5, jj))
                    for d in range(2):
                        msk = vcol("maskf" if d == 0 else "maskb", 0, 8)
                        S.op("dve", I("tensor_scalar", out=PE_[d], in0=G3[:, :, 2 + d], scalar1=-1.0, scalar2=None, op0=ALU.add),
                             reads=[("gath", pj), ("pe", d)], writes=[("pe", d)])
                        S.op("dve", I("tensor_tensor", out=PE_[d], in0=PE_[d], in1=msk, op=ALU.mult),
                             reads=[("pe", d)], writes=[("pe", d)])
                        S.op("dve", I("tensor_scalar", out=PE_[d], in0=PE_[d], scalar1=1.0, scalar2=None, op0=ALU.add),
                             reads=[("pe", d)], writes=[("pe", d)])
                        S.op("dve", I("tensor_tensor", out=FE_[d], in0=G3[:, :, d], in1=msk, op=ALU.mult),
                             reads=[("gath", pj), ("fe", d)], writes=[("fe", d)])
                    S.op("dve", I("tensor_tensor_scan", out=FOLD[0], data0=PE_[0], data1=FE_[0],
                                  initial=Ud[0][:, CTX - 1:CTX], op0=ALU.mult, op1=ALU.add),
                         reads=[("pe", 0), ("fe", 0), ("fold", 0)] + UK(0, [0]), writes=[("fold", 0)])
                    S.op("dve", I("tensor_tensor_scan", out=FOLD[1][:, ::-1], data0=PE_[1][:, ::-1], data1=FE_[1][:, ::-1],
                                  initial=Ud[1][:, 0:1], op0=ALU.mult, op1=ALU.add),
                         reads=[("pe", 1), ("fe", 1), ("fold", 1)] + UK(1, [0]), writes=[("fold", 1)])
                    S.op("dve", I("tensor_tensor_scan", out=Ud[0][:, CTX:NT], data0=Ad[0][:, CTX:NT], data1=Ud[0][:, CTX:NT],
                                  initial=FOLD[0][:, 7:8], op0=ALU.mult, op1=ALU.add),
                         reads=AK(0, lat) + UK(0, lat) + [("fold", 0)], writes=UK(0, lat))
                    S.op("dve", I("tensor_tensor_scan", out=Ud[1][:, CTX:NT][:, ::-1], data0=Ad[1][:, CTX:NT][:, ::-1],
                                  data1=Ud[1][:, CTX:NT][:, ::-1], initial=FOLD[1][:, 0:1], op0=ALU.mult, op1=ALU.add),
                         reads=AK(1, lat) + UK(1, lat) + [("fold", 1)], writes=UK(1, lat))
                    tl = [(ti, t) for ti, t in enumerate(TILES) if not (last and ti == 0)]
                    bzs, ths, sss = {}, {}, {}
                    for ti, (t0, N, RL) in tl:
                        bzs[ti] = psalloc()
                        mm8(bzs[ti], N, s5, (lambda kc, t0=t0, N=N: H[:, kc, t0:t0 + N]), hkeys(ti))
                    for ti, (t0, N, RL) in tl:
                        ths[ti] = talloc()
                        S.op("act", I("activation", out=tap(ths[ti], N), in_=PS[:, bzs[ti], 0:N], func=AF.Tanh, scale=0.5),
                             reads=[("ps", bzs[ti])], writes=[("t", ths[ti])])
                    for ti, (t0, N, RL) in tl:
                        S.op("dve", I("scalar_tensor_tensor", out=tap(ths[ti], N), in0=tap(ths[ti], N), scalar=1.0,
                                      in1=PS[:, bzs[ti], 0:N], op0=ALU.add, op1=ALU.mult),
                             reads=[("t", ths[ti]), ("ps", bzs[ti])], writes=[("t", ths[ti])])
                    for ti, (t0, N, RL) in tl:
                        sss[ti] = talloc()
                        S.op("pool", I("tensor_tensor", out=tap(sss[ti], N), in0=Ud[0][:, t0:t0 + N], in1=Ud[1][:, t0:t0 + N],
                                       op=ALU.add),
                             reads=[("U", pj, 0, ti), ("U", pj, 1, ti)], writes=[("t", sss[ti])])
                    for ti, (t0, N, RL) in tl:
                        S.op("dve", I("scalar_tensor_tensor", out=YB[:, jj, t0:t0 + N], in0=tap(ths[ti], N), scalar=0.5,
                                      in1=tap(sss[ti], N), op0=ALU.mult, op1=ALU.mult),
                             reads=[("t", ths[ti]), ("t", sss[ti])], writes=[("yb", jj, ti)])
                    wrelease(1)

        def stage_conv(l, last):
            for j in range(8):
                sv = wacquire(("in", l, 0, j))
                sc = wacquire(("in", l, 2, j))
                sb = wacquire(("in", l, 1, j))
                sz_ = wacquire(("in", l, 3, j))
                w3 = lambda t: vcol(("w3", l), t * 8 + j)
                for ti, (t0, N, RL) in enumerate(TILES):
                    if last and ti == 0:
                        continue
                    rhs = (lambda kc, t0=t0, N=N: H[:, kc, t0:t0 + N])
                    bv, bc, bb, bz = psalloc(), psalloc(), psalloc(), psalloc()
                    mm8(bv, N, sv, rhs, hkeys(ti))
                    mm8(bc, N, sc, rhs, hkeys(ti))
                    mm8(bb, N, sb, rhs, hkeys(ti))
                    mm8(bz, N, sz_, rhs, hkeys(ti))
                    tv, tg, tc, ts, tb_ = talloc(), talloc(), talloc(), talloc(), talloc()
                    S.op("act", I("activation", out=tap(tv, N), in_=PS[:, bv, 0:N], func=AF.Copy),
                         reads=[("ps", bv)], writes=[("t", tv)])
                    S.op("act", I("activation", out=tap(tb_, N), in_=PS[:, bb, 0:N], func=AF.Copy),
                         reads=[("ps", bb)], writes=[("t", tb_)])
                    S.op("act", I("activation", out=tap(ts, N), in_=PS[:, bz, 0:N], func=AF.Silu),
                         reads=[("ps", bz)], writes=[("t", ts)])
                    S.op("dve", I("tensor_tensor", out=tap(tg, N), in0=PS[:, bc, 0:N], in1=tap(tv, N), op=ALU.mult),
                         reads=[("ps", bc), ("t", tv)], writes=[("t", tg)])
                    S.op("act", I("activation", out=tap(tc, N), in_=tap(tg, N), func=AF.Identity, scale=w3(1)),
                         reads=[("t", tg)], writes=[("t", tc)])
                    g3 = tap(tg, N).rearrange("p (r w) -> p r w", w=RL)
                    c3 = tap(tc, N).rearrange("p (r w) -> p r w", w=RL)
                    for tapi, (osl, isl) in ((0, (slice(1, RL), slice(0, RL - 1))),
                                             (2, (slice(0, RL - 1), slice(1, RL)))):
                        S.op("dve", I("scalar_tensor_tensor", out=c3[:, :, osl], in0=g3[:, :, isl], scalar=w3(tapi),
                                      in1=c3[:, :, osl], op0=ALU.mult, op1=ALU.add),
                             reads=[("t", tg), ("t", tc)], writes=[("t", tc)])
                    S.op("dve", I("tensor_tensor", out=tap(tc, N), in0=tap(tc, N), in1=tap(tb_, N), op=ALU.mult),
                         reads=[("t", tb_), ("t", tc)], writes=[("t", tc)])
                    S.op("dve", I("tensor_tensor", out=YA[:, j, t0:t0 + N], in0=tap(tc, N), in1=tap(ts, N), op=ALU.mult),
                         reads=[("t", tc), ("t", ts)], writes=[("ya", j, ti)])
                wrelease(4)

        def stage_merge(l, last):
            for jo in range(8):
                s6 = wacquire(("in", l, 6, jo))
                s7 = wacquire(("in", l, 7, jo))
                sa = wacquire(("oa", l, jo))
                sb = wacquire(("ob", l, jo))
                for ti, (t0, N, RL) in enumerate(TILES):
                    if last and ti == 0:
                        continue
                    bma, bmb, bpa, bpb = psalloc(), psalloc(), psalloc(), psalloc()
                    mm8(bma, N, s6, (lambda kc, t0=t0, N=N: H[:, kc, t0:t0 + N]), hkeys(ti))
                    mm8(bmb, N, s7, (lambda kc, t0=t0, N=N: H[:, kc, t0:t0 + N]), hkeys(ti))
                    mm8(bpa, N, sa, (lambda kc, t0=t0, N=N: YA[:, kc, t0:t0 + N]), [("ya", kc, ti) for kc in range(8)])
                    mm8(bpb, N, sb, (lambda kc, t0=t0, N=N: YB[:, kc, t0:t0 + N]), [("yb", kc, ti) for kc in range(8)])
                    ta, tb = talloc(), talloc()
                    S.op("act", (I("activation", out=tap(ta, N), in_=PS[:, bma, 0:N], func=AF.Sigmoid)),
                         reads=[("ps", bma)], writes=[("t", ta)])
                    S.op("act", (I("activation", out=tap(tb, N), in_=PS[:, bmb, 0:N], func=AF.Sigmoid)),
                         reads=[("ps", bmb)], writes=[("t", tb)])
                    S.op("dve", (I("tensor_tensor",
                        out=tap(ta, N), in0=PS[:, bpa, 0:N], in1=tap(ta, N), op=ALU.mult)),
                        reads=[("ps", bpa), ("t", ta)], writes=[("t", ta)])
                    S.op("dve", (I("tensor_tensor",
                        out=tap(tb, N), in0=PS[:, bpb, 0:N], in1=tap(tb, N), op=ALU.mult)),
                        reads=[("ps", bpb), ("t", tb)], writes=[("t", tb)])
                    S.op("dve", (I("tensor_tensor",
                        out=MG[:, jo, t0:t0 + N], in0=tap(ta, N), in1=tap(tb, N), op=ALU.add)),
                        reads=[("t", ta), ("t", tb)], writes=[("mg", jo, ti)])
                wrelease(4)

        def stage_out(l, last):
            src = xT if l == 0 else xs
            for jo in range(8):
                S.dma("sync", (I("dma_start", out=X[:, jo, :], in_=src[jo * 128:(jo + 1) * 128, :])),
                      reads=[("xs", jo)], writes=[("X", jo, ti) for ti in range(5)], sem=f"x{jo}")
            for jo in range(8):
                so = wacquire(("o", l, jo))
                for ti, (t0, N, RL) in enumerate(TILES):
                    if last and ti == 0:
                        continue
                    n = 1 if ti == 0 else 0
                    bo = psalloc()
                    mm8(bo, N, so, (lambda kc, t0=t0, N=N: MG[:, kc, t0:t0 + N]), [("mg", kc, ti) for kc in range(8)])
                    S.op("dve", (I("scalar_tensor_tensor",
                        out=X[:, jo, t0:t0 + N], in0=PS[:, bo, 0:N], scalar=mod_col(l, 2, jo, n),
                        in1=X[:, jo, t0:t0 + N], op0=ALU.mult, op1=ALU.add)),
                        reads=[("ps", bo), ("X", jo, ti)], writes=[("X", jo, ti)])
                if not last:
                    S.dma("act", (I("dma_start", out=xs[jo * 128:(jo + 1) * 128, :], in_=X[:, jo, :])),
                          reads=[("X", jo, ti) for ti in range(5)], writes=[("xs", jo)], sem=f"x{jo}")
                wrelease(1)

        for l in range(depth):
            last = (l == DEPTH - 1)
            stage_norm(l, False)
            S.barrier()
            stage_lru(l, last)
            S.barrier()
            stage_conv(l, last)
            stage_merge(l, last)
            S.barrier()
            stage_out(l, last)
        if final_norm:
            stage_norm(None, True)
        for kc in range(8):
            S.dma("sync", (I("dma_start", out=outT[kc * 128:(kc + 1) * 128, :], in_=X[:, kc, CTX:NT])),
                  reads=[("X", kc, ti) for ti in range(1, 5)], writes=[("out", kc)], sem="out")
        S.barrier()

        sem_names = ["pe", "act", "dve", "pool"] + sorted(S.dcum.keys())
        sems = {n: es.enter_context(nc.semaphore("s_" + n)) for n in sem_names}
        block = es.enter_context(nc.Block())

        def emit(eng_name):
            def body(e):
                for waits, fn, inc in S.q[eng_name]:
                    for s, v in waits:
                        e.wait_ge(sems[s], v)
                    if fn is None:
                        continue
                    ins = None
                    for m, kw in fn:
                        ins = getattr(e, m)(**kw)
                    ins.then_inc(sems[inc[0]], inc[1])
            return body

        block.tensor(emit("pe"))
        block.scalar(emit("act"))
        block.vector(emit("dve"))
        block.gpsimd(emit("pool"))
        block.sync(emit("sync"))
    return nc


_NC_CACHE = {}


def _prep_inputs(inputs):
    f = lambda k: np.asarray(inputs[k], dtype=np.float32)
    x, c, ctx, c_ctx = f("x"), f("c"), f("ctx"), f("c_ctx")
    shared = {k: np.ascontiguousarray(f(k)) for k in ("w_in", "lru_wr", "lru_wi", "w_out_a", "w_out_b", "w_o")}
    per_layer = []
    for l in range(DEPTH):
        per_layer += [_lay(f("norm_g")[l]), _lay(f("b_ada")[l].reshape(3, D)), _lay(f("conv3_w")[l]),
                      _lay(f("conv4_w")[l]), _lay(f("conv4_b")[l]), _lay(f("lru_br")[l]), _lay(f("lru_bi")[l]),
                      _lay(f("lru_lambda")[l])]
    in_maps = []
    for r in range(NCORES):
        b, q = divmod(r, QPB)
        xt = np.concatenate([ctx[b], x[b, q * LAT:(q + 1) * LAT]], axis=0)
        xT = np.ascontiguousarray(xt.T)
        maskf = np.zeros((128, NCORES), np.float32)
        maskb = np.zeros((128, NCORES), np.float32)
        for rr in range(NCORES):
            if rr // QPB == b and rr < r:
                maskf[:, rr] = 1.0
            if rr // QPB == b and rr > r:
                maskb[:, rr] = 1.0
        consts = np.zeros((128, 6), np.float32)
        consts[:, 0] = 1.0
        consts[:, 1] = RMS_EPS
        consts[:, 3] = 0.25
        consts[:, 4 + b] = 1.0
        vec = np.concatenate(per_layer + [_lay(f("final_g")), _lay(c[0]), _lay(c[1]), _lay(c_ctx), maskf, maskb, consts],
                             axis=1)
        assert vec.shape == (128, NV), vec.shape
        m = {"xT": xT, "vec": np.ascontiguousarray(vec),
             "w_ada_s": np.ascontiguousarray(f("w_ada")[:, :, r * 384:(r + 1) * 384])}
        m.update(shared)
        in_maps.append(m)
    return in_maps


def kernel(**inputs):
    if "nc" not in _NC_CACHE:
        _NC_CACHE["nc"] = build_nc()
    nc = _NC_CACHE["nc"]
    in_maps = _prep_inputs(inputs)
    res = run_bass_kernel_spmd(nc, in_maps, core_ids=list(range(NCORES)))
    out = np.empty((NB, SEQ, D), np.float32)
    for r in range(NCORES):
        b, q = divmod(r, QPB)
        out[b, q * LAT:(q + 1) * LAT, :] = np.asarray(res.results[r]["outT"]).T
    return out
```

```python
import contextlib
import numpy as np
import concourse.bass as bass
import concourse.mybir as mybir
from concourse.bass_utils import run_bass_kernel_spmd

F32 = mybir.dt.float32
BF16 = mybir.dt.bfloat16
AF = mybir.ActivationFunctionType
ALU = mybir.AluOpType

D = 1024
NB = 2
SEQ = 8192
DEPTH = 2
CTX = 256
GW = 64
NCORES = 8
QPB = 4
LAT = SEQ // QPB
NT = CTX + LAT
KC = 8
RMS_EPS = 1e-6
TILES = [(0, 256, 256)] + [(CTX + 512 * i, 512, GW) for i in range(4)]
NSLOT = 8
NTEMP = 10

_VL = {}
_off = 0
for _l in range(DEPTH):
    for _name, _n in (("g", 8), ("bada", 24), ("w3", 24), ("w4", 32), ("b4", 8), ("br", 16), ("bi", 16), ("lam", 16)):
        _VL[(_name, _l)] = _off
        _off += _n
for _name, _n in (("fg", 8), ("c0", 8), ("c1", 8), ("cctx", 8), ("maskf", 8), ("maskb", 8), ("one", 1), ("eps", 1),
                  ("zero", 1), ("quarter", 1), ("sel0", 1), ("sel1", 1)):
    _VL[_name] = _off
    _off += _n
NV = _off

LD = 128
DER_SCC = DEPTH * LD
NDER = DER_SCC + 24


def _lay(v):
    v = np.asarray(v, dtype=np.float32).reshape(-1, 8, 128)
    return np.ascontiguousarray(v.transpose(2, 0, 1).reshape(128, -1))


def I(method, **kw):
    return [(method, kw)]


class Sched:
    ENGS = ("pe", "act", "dve", "pool", "sync")

    def __init__(self):
        self.q = {e: [] for e in self.ENGS}
        self.cnt = {e: 0 for e in self.ENGS}
        self.lw = {}
        self.rd = {}
        self.seen = {e: {} for e in self.ENGS}
        self.dcum = {}
        self.same_engine_sync = True
        self.unread = set()
        self.wcount = {}

    def _waits(self, eng, reads, writes):
        toks = []
        for k in reads:
            t = self.lw.get(k)
            if t is not None:
                toks.append(t)
        for k in writes:
            t = self.lw.get(k)
            if t is not None:
                toks.append(t)
            toks.extend(self.rd.get(k, ()))
        need = {}
        for s, v in toks:
            if s == eng and (eng == "pe" or not self.same_engine_sync):
                continue
            if need.get(s, 0) < v:
                need[s] = v
        out = []
        for s, v in need.items():
            if self.seen[eng].get(s, 0) >= v:
                continue
            self.seen[eng][s] = v
            out.append((s, v))
        return out

    def _commit(self, tok, reads, writes):
        for k in reads:
            self.unread.discard(k)
        for k in writes:
            self.unread.add(k)
            self.wcount[k] = self.wcount.get(k, 0) + 1
        for k in writes:
            self.lw[k] = tok
            self.rd[k] = []
        for k in reads:
            if k not in writes:
                self.rd.setdefault(k, []).append(tok)

    def op(self, eng, fn, reads=(), writes=()):
        reads, writes = list(reads), list(writes)
        w = self._waits(eng, reads, writes)
        self.cnt[eng] += 1
        tok = (eng, self.cnt[eng])
        self.q[eng].append((w, fn, (eng, 1)))
        self._commit(tok, reads, writes)
        return tok

    def dma(self, queue, fn, reads, writes, sem, n=1, inc=16):
        reads, writes = list(reads), list(writes)
        w = self._waits(queue, reads, writes)
        self.dcum[sem] = self.dcum.get(sem, 0) + inc * n
        tok = (sem, self.dcum[sem])
        self.q[queue].append((w, fn, (sem, inc)))
        self._commit(tok, reads, writes)
        return tok

    def barrier(self):
        toks = [(e, c) for e, c in self.cnt.items() if c > 0] + list(self.dcum.items())
        for eng in self.ENGS:
            w = []
            for s, v in toks:
                if s == eng:
                    continue
                if self.seen[eng].get(s, 0) >= v:
                    continue
                self.seen[eng][s] = v
                w.append((s, v))
            if w:
                self.q[eng].append((w, None, None))
        self.lw.clear()
        self.rd.clear()
        self.unread.clear()


def build_nc(depth=DEPTH, final_norm=True):
    nc = bass.Bass("TRN2", target_bir_lowering=False)
    S = Sched()

    xT = nc.dram_tensor("xT", [D, NT], F32, kind="ExternalInput").ap()
    vec_d = nc.dram_tensor("vec", [128, NV], F32, kind="ExternalInput").ap()
    w_ada_s = nc.dram_tensor("w_ada_s", [DEPTH, D, 384], F32, kind="ExternalInput").ap()
    w_in_d = nc.dram_tensor("w_in", [DEPTH, D, 8 * D], F32, kind="ExternalInput").ap()
    wr_d = nc.dram_tensor("lru_wr", [DEPTH, 2, 8, 128, 128], F32, kind="ExternalInput").ap()
    wi_d = nc.dram_tensor("lru_wi", [DEPTH, 2, 8, 128, 128], F32, kind="ExternalInput").ap()
    woa_d = nc.dram_tensor("w_out_a", [DEPTH, D, D], F32, kind="ExternalInput").ap()
    wob_d = nc.dram_tensor("w_out_b", [DEPTH, D, D], F32, kind="ExternalInput").ap()
    wo_d = nc.dram_tensor("w_o", [DEPTH, D, D], F32, kind="ExternalInput").ap()
    outT = nc.dram_tensor("outT", [D, LAT], F32, kind="ExternalOutput").ap()
    xs = nc.dram_tensor("xs", [D, NT], F32).ap()
    cc_ada_in = nc.dram_tensor("cc_ada_in", [128, 18], F32)
    cc_ada_out = nc.dram_tensor("cc_ada_out", [NCORES * 128, 18], F32)
    cc_in = [[nc.dram_tensor(f"ccin_{l}_{j}", [128, 4], F32) for j in range(8)] for l in range(DEPTH)]
    cc_out = [[nc.dram_tensor(f"ccout_{l}_{j}", [NCORES * 128, 4], F32) for j in range(8)] for l in range(DEPTH)]

    es = contextlib.ExitStack()
    with es:
        REG_H = es.enter_context(nc.sbuf_tensor("reg_h", [128, 8 * NT], BF16))
        REG_M = es.enter_context(nc.sbuf_tensor("reg_m", [128, 8 * NT], BF16))
        BIG = es.enter_context(nc.sbuf_tensor("reg_big", [128, 21888], F32))
        WR = es.enter_context(nc.sbuf_tensor("wring", [128, NSLOT * 1024], BF16))
        TMP = es.enter_context(nc.sbuf_tensor("tmps", [128, NTEMP * 512], F32))
        VEC = es.enter_context(nc.sbuf_tensor("vecs", [128, NV], F32))
        DER = es.enter_context(nc.sbuf_tensor("der", [128, NDER], F32))
        SM = es.enter_context(nc.sbuf_tensor("sm", [128, 256], F32))
        ADG = es.enter_context(nc.sbuf_tensor("adg", [128, NCORES * 18], F32))
        JUNK = es.enter_context(nc.sbuf_tensor("junk", [128, NT], F32))
        ONES = es.enter_context(nc.sbuf_tensor("ones", [128, 128], F32))
        PS = es.enter_context(nc.psum_tensor("ps", [128, 8, 512], F32))

        H = REG_H[:, :].rearrange("p (k t) -> p k t", k=8)
        MG = REG_M[:, :].rearrange("p (k t) -> p k t", k=8)
        MF = REG_M[:, :].bitcast(F32)
        ADAST = MF[:, 0:6144].rearrange("p (l k n) -> p l k n", l=DEPTH, k=8)
        X = BIG[:, 0:8 * NT].rearrange("p (k t) -> p k t", k=8)
        YB = BIG[:, 0:9216].bitcast(BF16).rearrange("p (k t) -> p k t", k=8)
        YA = BIG[:, 9216:18432].bitcast(BF16).rearrange("p (k t) -> p k t", k=8)
        A_ = [BIG[:, 9216:11520], BIG[:, 11520:13824]]
        U_ = [BIG[:, 13824:16128], BIG[:, 16128:18432]]
        XC = BIG[:, 18432:20736]
        XCB = BIG[:, 20736:21888].bitcast(BF16)
        WS = WR[:, :].rearrange("p (s k m) -> p s k m", s=NSLOT, k=8)

        def vcol(name, i=0, n=1):
            o = _VL[name] + i
            return VEC[:, o:o + n]

        def dcol(o, n=1):
            return DER[:, o:o + n]

        def mod_col(l, which, kc, n):
            return dcol(l * LD + n * 24 + which * 8 + kc)

        def gs_col(l, kc, n):
            return dcol(l * LD + 48 + n * 8 + kc)

        def c1_col(l, d, j):
            return dcol(l * LD + 64 + d * 8 + j)

        def c2_col(l, d, j):
            return dcol(l * LD + 80 + d * 8 + j)

        PKS = [SM[:, 0:4], SM[:, 202:206]]
        G3S = [SM[:, 4:36].rearrange("p (r f) -> p r f", f=4), SM[:, 206:238].rearrange("p (r f) -> p r f", f=4)]
        AD18 = SM[:, 160:178]
        T24 = SM[:, 178:202]
        LSETS = [dict(A=[BIG[:, 9216:11520], BIG[:, 11520:13824]], U=[BIG[:, 13824:16128], BIG[:, 16128:18432]]),
                 dict(A=[MF[:, 0:2304], MF[:, 2304:4608]], U=[MF[:, 4608:6912], MF[:, 6912:9216]])]
        PE_ = [SM[:, 36:44], SM[:, 52:60]]
        FE_ = [SM[:, 44:52], SM[:, 60:68]]
        FOLD = [SM[:, 68:76], SM[:, 76:84]]
        RACC = [SM[:, 84:88], SM[:, 88:92]]
        RSUM = [SM[:, 92:93], SM[:, 93:94]]
        JUNK4 = SM[:, 94:98]
        T8 = SM[:, 102:110]
        T16A = SM[:, 110:126]
        T16B = SM[:, 126:142]

        st = {"ps": 0, "t": 0}

        resv = {}

        def _alloc(kind, n):
            for i in range(n):
                b = (st[kind] + i) % n
                key = (kind, b)
                if key in S.unread:
                    continue
                if key in resv and S.wcount.get(key, 0) <= resv[key]:
                    continue
                resv[key] = S.wcount.get(key, 0)
                st[kind] = (b + 1) % n
                return b
            raise RuntimeError(f"no free {kind} buffer: every one holds a value with no emitted reader")

        def psalloc():
            return _alloc("ps", 8)

        def talloc():
            return _alloc("t", NTEMP)

        def tap(s, N):
            return TMP[:, s * 512:s * 512 + N]

        wplan = []
        for l in range(depth):
            wplan += [("in", l, 4, 0), ("gate", l, 0)]
            for j in range(9):
                if j > 0:
                    wplan += [("in", l, 5, j - 1)]
                if j + 1 < 8:
                    wplan += [("in", l, 4, j + 1), ("gate", l, j + 1)]
            for j in range(8):
                wplan += [("in", l, 0, j), ("in", l, 2, j), ("in", l, 1, j), ("in", l, 3, j)]
            for j in range(8):
                wplan += [("in", l, 6, j), ("in", l, 7, j), ("oa", l, j), ("ob", l, j)]
            for j in range(8):
                wplan += [("o", l, j)]
        wst = {"issued": 0, "consumed": 0, "released": 0}

        def issue_wload(idx):
            spec = wplan[idx]
            slot = idx % NSLOT
            kind = spec[0]
            if kind == "gate":
                _, l, j = spec
                srcs = [(WS[:, slot, 0:2, :], wr_d[l, :, j].rearrange("d k m -> k d m")),
                        (WS[:, slot, 2:4, :], wi_d[l, :, j].rearrange("d k m -> k d m"))]
            else:
                if kind == "in":
                    _, l, c, j = spec
                    src = w_in_d[l].rearrange("(kc p) n -> p kc n", p=128)[:, :, c * D + j * 128:c * D + (j + 1) * 128]
                else:
                    _, l, j = spec
                    base = {"oa": woa_d, "ob": wob_d, "o": wo_d}[kind]
                    src = base[l].rearrange("(kc p) n -> p kc n", p=128)[:, :, j * 128:(j + 1) * 128]
                srcs = [(WS[:, slot, :, :], src)]
            for i, (dst, src) in enumerate(srcs):
                S.dma("pool", (I("dma_start", out=dst, in_=src)),
                      reads=[], writes=[("w", slot)] if i == 0 else [], sem=f"w{slot}")
            if len(srcs) > 1:
                S.lw[("w", slot)] = (f"w{slot}", S.dcum[f"w{slot}"])

        def wpump():
            lim = min(len(wplan), wst["released"] + NSLOT)
            while wst["issued"] < lim:
                issue_wload(wst["issued"])
                wst["issued"] += 1

        def wacquire(spec):
            c = wst["consumed"]
            assert wplan[c] == spec, (wplan[c], spec)
            assert c < wst["released"] + NSLOT
            wpump()
            wst["consumed"] = c + 1
            return c % NSLOT

        def wrelease(n):
            wst["released"] += n
            assert wst["released"] <= wst["consumed"]
            wpump()

        def mm8(bank, N, slot, rhs_fn, rkeys):
            ins = []
            for kc in range(8):
                ins += I("matmul", out=PS[:, bank, 0:N], lhsT=WS[:, slot, kc, :], rhs=rhs_fn(kc),
                         start=(kc == 0), stop=(kc == 7))
            S.op("pe", ins, reads=[("w", slot)] + rkeys, writes=[("ps", bank)])

        S.dma("sync", I("dma_start", out=VEC[:, :], in_=vec_d), reads=[], writes=[("vec",)], sem="vecl")
        S.dma("sync", I("dma_start", out=ADAST, in_=w_ada_s.rearrange("l (kc p) n -> p l kc n", p=128)),
              reads=[], writes=[("adast",)], sem="adal")
        S.op("dve", I("memset", ap=ONES[:, :], constant=1.0), writes=[("ones",)])
        for c in range(8):
            S.dma("sync", I("dma_start", out=X[:, c, :], in_=xT[c * 128:(c + 1) * 128, :]),
                  reads=[], writes=[("X", c, ti) for ti in range(5)], sem=f"x{c}")
        SCC = DER[:, DER_SCC:DER_SCC + 24].rearrange("p (k n) -> p k n", n=3)
        for n, nm in enumerate(("c0", "c1", "cctx")):
            S.op("act", I("activation", out=SCC[:, :, n], in_=vcol(nm, 0, 8), func=AF.Silu),
                 reads=[("vec",)], writes=[("scc", n)])
        fn = []
        for l in range(DEPTH):
            for o in range(3):
                for kc in range(8):
                    fn += I("matmul", out=PS[:, 0, (l * 3 + o) * 3:(l * 3 + o) * 3 + 3],
                            lhsT=ADAST[:, l, kc, o * 128:(o + 1) * 128],
                            rhs=DER[:, DER_SCC + kc * 3:DER_SCC + kc * 3 + 3], start=(kc == 0), stop=(kc == 7))
        S.op("pe", fn, reads=[("adast",)] + [("scc", n) for n in range(3)], writes=[("ps", 0)])
        S.op("dve", I("tensor_copy", out=AD18, in_=PS[:, 0, 0:18]), reads=[("ps", 0)], writes=[("ad18",)])
        S.dma("pool", I("dma_start", out=cc_ada_in.ap(), in_=AD18), reads=[("ad18",)], writes=[("ccadain",)], sem="pk0")
        S.dma("pool", I("collective_compute", kind="AllGather", op=ALU.bypass, replica_groups=[list(range(NCORES))],
                        ins=[cc_ada_in.ap().opt()], outs=[cc_ada_out.ap().opt()]),
              reads=[("ccadain",)], writes=[("ccadaout",)], sem="cc", inc=1)
        S.dma("pool", I("dma_start", out=ADG[:, :].rearrange("p (r f) -> p r f", r=NCORES),
                        in_=cc_ada_out.ap().rearrange("(r p) f -> p r f", p=128)),
              reads=[("ccadaout",)], writes=[("adg",)], sem="pk0")
        GV = ADG[:, :].rearrange("p (r l o n) -> p l n r o", r=NCORES, l=DEPTH, o=3, n=3)
        T24v = T24.rearrange("p (r o) -> p r o", o=3)
        for l in range(DEPTH):
            base = l * LD
            bada3 = vcol(("bada", l), 0, 24).rearrange("p (r o) -> p r o", o=3)
            S.op("dve", I("tensor_scalar", out=T24v, in0=GV[:, l, 0], scalar1=vcol("sel0"), scalar2=None, op0=ALU.mult),
                 reads=[("adg",), ("vec",), ("t24",)], writes=[("t24",)])
            S.op("dve", I("scalar_tensor_tensor", out=T24v, in0=GV[:, l, 1], scalar=vcol("sel1"), in1=T24v,
                          op0=ALU.mult, op1=ALU.add), reads=[("adg",), ("t24",)], writes=[("t24",)])
            S.op("dve", I("tensor_tensor", out=DER[:, base:base + 24], in0=T24, in1=vcol(("bada", l), 0, 24), op=ALU.add),
                 reads=[("t24",)], writes=[("mod", l, 0)])
            S.op("dve", I("tensor_tensor", out=DER[:, base + 24:base + 48].rearrange("p (r o) -> p r o", o=3),
                          in0=GV[:, l, 2], in1=bada3, op=ALU.add), reads=[("adg",), ("vec",)], writes=[("mod", l, 1)])
            for n in range(2):
                mb = base + n * 24
                S.op("dve", I("tensor_scalar", out=T8, in0=DER[:, mb + 8:mb + 16], scalar1=1.0, scalar2=None, op0=ALU.add),
                     reads=[("mod", l, n), ("t8",)], writes=[("t8",)])
                S.op("dve", I("tensor_tensor", out=DER[:, base + 48 + n * 8:base + 56 + n * 8], in0=T8,
                              in1=vcol(("g", l), 0, 8), op=ALU.mult), reads=[("t8",), ("vec",)], writes=[("gs", l, n)])
            S.op("act", I("activation", out=T16A, in_=vcol(("lam", l), 0, 16), func=AF.Exp, scale=-1.0),
                 reads=[("vec",), ("t16a",)], writes=[("t16a",)])
            S.op("dve", I("tensor_scalar", out=T16B, in0=T16A, scalar1=-0.25, scalar2=1.0 / 3.0, op0=ALU.mult, op1=ALU.add),
                 reads=[("t16a",), ("t16b",)], writes=[("t16b",)])
            for cst in (-0.5, 1.0):
                S.op("dve", I("tensor_tensor", out=T16B, in0=T16B, in1=T16A, op=ALU.mult),
                     reads=[("t16a",), ("t16b",)], writes=[("t16b",)])
                S.op("dve", I("tensor_scalar", out=T16B, in0=T16B, scalar1=cst, scalar2=None, op0=ALU.add),
                     reads=[("t16b",)], writes=[("t16b",)])
            S.op("dve", I("tensor_tensor", out=T16B, in0=T16B, in1=T16A, op=ALU.mult),
                 reads=[("t16a",), ("t16b",)], writes=[("t16b",)])
            S.op("dve", I("tensor_scalar", out=DER[:, base + 64:base + 80], in0=T16B, scalar1=-4.0, scalar2=None, op0=ALU.mult),
                 reads=[("t16b",)], writes=[("hc1", l)])
            S.op("dve", I("tensor_scalar", out=DER[:, base + 80:base + 96], in0=T16B, scalar1=-4.0 * LAT, scalar2=None,
                          op0=ALU.mult), reads=[("t16b",)], writes=[("hc1n", l)])
            S.op("dve", I("tensor_scalar", out=DER[:, base + 96:base + 112], in0=vcol(("br", l), 0, 16), scalar1=0.5,
                          scalar2=None, op0=ALU.mult), reads=[("vec",)], writes=[("hbr", l)])
            S.op("dve", I("tensor_scalar", out=DER[:, base + 112:base + 128], in0=vcol(("bi", l), 0, 16), scalar1=0.5,
                          scalar2=None, op0=ALU.mult), reads=[("vec",)], writes=[("hbi", l)])
        S.barrier()

        def stage_norm(l, final):
            for ti, (t0, N, RL) in enumerate(TILES):
                if final and ti == 0:
                    continue
                n = 1 if ti == 0 else 0
                bank = psalloc()
                for kc in range(8):
                    s = talloc()
                    S.op("act", (I("activation",
                        out=tap(s, N), in_=X[:, kc, t0:t0 + N], func=AF.Square)),
                        reads=[("X", kc, ti)], writes=[("t", s)])
                    S.op("pe", (I("matmul",
                        out=PS[:, bank, 0:N], lhsT=ONES[:, :], rhs=tap(s, N), start=(kc == 0), stop=(kc == 7))),
                        reads=[("t", s), ("ones",)], writes=[("ps", bank)])
                sd = talloc()
                S.op("act", (I("activation",
                    out=tap(sd, N), in_=PS[:, bank, 0:N], func=AF.Sqrt, scale=1.0 / D, bias=vcol("eps"))),
                    reads=[("ps", bank)], writes=[("t", sd)])
                S.op("dve", (I("reciprocal", out=tap(sd, N), in_=tap(sd, N))),
                     reads=[("t", sd)], writes=[("t", sd)])
                for kc in range(8):
                    if final:
                        S.op("dve", (I("scalar_tensor_tensor",
                            out=X[:, kc, t0:t0 + N], in0=X[:, kc, t0:t0 + N], scalar=vcol("fg", kc),
                            in1=tap(sd, N), op0=ALU.mult, op1=ALU.mult)),
                            reads=[("X", kc, ti), ("t", sd)], writes=[("X", kc, ti)])
                    else:
                        s = talloc()
                        S.op("dve", (I("scalar_tensor_tensor",
                            out=tap(s, N), in0=X[:, kc, t0:t0 + N], scalar=gs_col(l, kc, n), in1=tap(sd, N),
                            op0=ALU.mult, op1=ALU.mult)),
                            reads=[("X", kc, ti), ("t", sd)], writes=[("t", s)])
                        S.op("act", (I("activation",
                            out=H[:, kc, t0:t0 + N], in_=tap(s, N), func=AF.Identity,
                            bias=mod_col(l, 0, kc, n))),
                            reads=[("t", s)], writes=[("h", kc, ti)])

        def hkeys(ti):
            return [("h", kc, ti) for kc in range(8)]

        def stage_lru(l, last):
            lat = [1, 2, 3, 4]
            hc1 = lambda d, b: dcol(l * LD + 64 + d * 8 + b)
            hc1n = lambda d, b: dcol(l * LD + 80 + d * 8 + b)
            ytl = [(ti, t) for ti, t in enumerate(TILES) if not (last and ti == 0)]
            gbanks = {}

            def front(b):
                s4 = wacquire(("in", l, 4, b))
                w4 = lambda t: vcol(("w4", l), t * 8 + b)
                bxs = []
                for ti, (t0, N, RL) in enumerate(TILES):
                    bx = psalloc()
                    bxs.append(bx)
                    mm8(bx, N, s4, (lambda kc, t0=t0, N=N: H[:, kc, t0:t0 + N]), hkeys(ti))
                for ti, (t0, N, RL) in enumerate(TILES):
                    S.op("act", I("activation", out=XC[:, t0:t0 + N], in_=PS[:, bxs[ti], 0:N], func=AF.Identity,
                                  scale=w4(2), bias=vcol(("b4", l), b)),
                         reads=[("ps", bxs[ti])], writes=[("XC", ti)])
                for tapi, sh_o, sh_i in ((0, 2, 0), (1, 1, 0), (3, 0, 1)):
                    for ti, (t0, N, RL) in enumerate(TILES):
                        xc3 = XC[:, t0:t0 + N].rearrange("p (r w) -> p r w", w=RL)
                        ps3 = PS[:, bxs[ti], 0:N].rearrange("p (r w) -> p r w", w=RL)
                        ln = RL - max(sh_o, sh_i)
                        osl = slice(sh_o, sh_o + ln)
                        isl = slice(sh_i, sh_i + ln)
                        S.op("dve", I("scalar_tensor_tensor", out=xc3[:, :, osl], in0=ps3[:, :, isl], scalar=w4(tapi),
                                      in1=xc3[:, :, osl], op0=ALU.mult, op1=ALU.add),
                             reads=[("ps", bxs[ti]), ("XC", ti)], writes=[("XC", ti)])
                wrelease(1)

            def back(b):
                sg = wacquire(("gate", l, b))
                for ti, (t0, N, RL) in enumerate(TILES):
                    S.op("act", I("activation", out=XCB[:, t0:t0 + N], in_=XC[:, t0:t0 + N], func=AF.Copy),
                         reads=[("XC", ti)], writes=[("xcb", ti)])
                gslot[b] = sg
                for ti in range(len(TILES)):
                    gate_mm(b, 0, 0, ti)

            gslot = {}

            def gate_mm(b, d, g, ti):
                t0, N, RL = TILES[ti]
                bk = psalloc()
                gbanks[(b, d, g, ti)] = bk
                sg = gslot[b]
                S.op("pe", I("matmul", out=PS[:, bk, 0:N], lhsT=WS[:, sg, 2 * g + d, :], rhs=XCB[:, t0:t0 + N],
                             start=True, stop=True),
                     reads=[("w", sg), ("xcb", ti)], writes=[("ps", bk)])

            GROUPS = [(0, 0), (0, 1), (1, 0), (1, 1)]
            front(0)
            back(0)
            for b in range(9):
                pj = b % 2
                Ad, Ud = LSETS[pj]["A"], LSETS[pj]["U"]
                AK = lambda d, tis, pj=pj: [("A", pj, d, ti) for ti in tis]
                UK = lambda d, tis, pj=pj: [("U", pj, d, ti) for ti in tis]
                if b < 8:
                    for gi, (d, g) in enumerate(GROUPS):
                        nxt = GROUPS[gi + 1] if gi + 1 < 4 else None
                        if nxt:
                            for ti in range(3):
                                gate_mm(b, nxt[0], nxt[1], ti)
                        for ti, (t0, N, RL) in enumerate(TILES):
                            bk = gbanks.pop((b, d, g, ti))
                            hb = dcol(l * LD + (96 if g == 0 else 112) + d * 8 + b)
                            if g == 0:
                                kw = dict(accum_out=RACC[d][:, ti - 1:ti]) if ti > 0 else {}
                                S.op("act", I("activation", out=Ad[d][:, t0:t0 + N], in_=PS[:, bk, 0:N], func=AF.Tanh,
                                              scale=0.5, bias=hb, **kw),
                                     reads=[("ps", bk)], writes=[("A", pj, d, ti)] + ([("racc", d, ti)] if ti > 0 else []))
                            else:
                                S.op("act", I("activation", out=Ud[d][:, t0:t0 + N], in_=PS[:, bk, 0:N], func=AF.Tanh,
                                              scale=0.5, bias=hb),
                                     reads=[("ps", bk)], writes=[("U", pj, d, ti)])
                            if nxt and ti + 3 < len(TILES):
                                gate_mm(b, nxt[0], nxt[1], ti + 3)
                        if g == 1:
                            for ti, (t0, N, RL) in enumerate(TILES):
                                S.op("dve", I("scalar_tensor_tensor", out=Ud[d][:, t0:t0 + N], in0=Ud[d][:, t0:t0 + N],
                                              scalar=1.0, in1=XC[:, t0:t0 + N], op0=ALU.add, op1=ALU.mult),
                                     reads=[("U", pj, d, ti), ("XC", ti)], writes=[("U", pj, d, ti)])
                    wrelease(1)
                    for d in range(2):
                        for ti, (t0, N, RL) in enumerate(TILES):
                            S.op("act", I("activation", out=Ad[d][:, t0:t0 + N], in_=Ad[d][:, t0:t0 + N], func=AF.Exp,
                                          scale=hc1(d, b), bias=hc1(d, b)),
                                 reads=[("A", pj, d, ti)], writes=[("A", pj, d, ti)])
                if b > 0:
                    jj = b - 1
                    s5 = wacquire(("in", l, 5, jj))
                    bzs, ths = {}, {}
                    for ti, (t0, N, RL) in ytl:
                        bzs[ti] = psalloc()
                        mm8(bzs[ti], N, s5, (lambda kc, t0=t0, N=N: H[:, kc, t0:t0 + N]), hkeys(ti))
                    wrelease(1)
                    for ti, (t0, N, RL) in ytl:
                        ths[ti] = talloc()
                        S.op("act", I("activation", out=tap(ths[ti], N), in_=PS[:, bzs[ti], 0:N], func=AF.Tanh, scale=0.5),
                             reads=[("ps", bzs[ti])], writes=[("t", ths[ti])])
                    for ti, (t0, N, RL) in ytl:
                        S.op("dve", I("scalar_tensor_tensor", out=JUNK[:, t0:t0 + N], in0=tap(ths[ti], N), scalar=1.0,
                                      in1=PS[:, bzs[ti], 0:N], op0=ALU.add, op1=ALU.mult),
                             reads=[("t", ths[ti]), ("ps", bzs[ti])], writes=[("junk", ti)])
                if b + 1 < 8:
                    front(b + 1)
                if b < 8:
                    for d in range(2):
                        for ti, (t0, N, RL) in enumerate(TILES):
                            tm = talloc()
                            S.op("act", I("activation", out=tap(tm, N), in_=Ad[d][:, t0:t0 + N], func=AF.Square),
                                 reads=[("A", pj, d, ti)], writes=[("t", tm)])
                            S.op("act", I("activation", out=tap(tm, N), in_=tap(tm, N), func=AF.Sqrt, scale=-0.25,
                                          bias=vcol("quarter")), reads=[("t", tm)], writes=[("t", tm)])
                            S.op("dve", I("tensor_tensor", out=Ud[d][:, t0:t0 + N], in0=Ud[d][:, t0:t0 + N], in1=tap(tm, N),
                                          op=ALU.mult),
                                 reads=[("U", pj, d, ti), ("t", tm)], writes=[("U", pj, d, ti)])
                if b + 1 < 8:
                    back(b + 1)
                if b > 0:
                    jj = b - 1
                    pq = jj % 2
                    Aq, Uq = LSETS[pq]["A"], LSETS[pq]["U"]
                    AQ = lambda d, tis, pq=pq: [("A", pq, d, ti) for ti in tis]
                    UQ = lambda d, tis, pq=pq: [("U", pq, d, ti) for ti in tis]
                    G3 = G3S[pq]
                    for d in range(2):
                        msk = vcol("maskf" if d == 0 else "maskb", 0, 8)
                        S.op("dve", I("tensor_scalar", out=PE_[d], in0=G3[:, :, 2 + d], scalar1=-1.0, scalar2=None, op0=ALU.add),
                             reads=[("gath", pq), ("pe", d)], writes=[("pe", d)])
                        S.op("dve", I("tensor_tensor", out=PE_[d], in0=PE_[d], in1=msk, op=ALU.mult),
                             reads=[("pe", d)], writes=[("pe", d)])
                        S.op("dve", I("tensor_scalar", out=PE_[d], in0=PE_[d], scalar1=1.0, scalar2=None, op0=ALU.add),
                             reads=[("pe", d)], writes=[("pe", d)])
                        S.op("dve", I("tensor_tensor", out=FE_[d], in0=G3[:, :, d], in1=msk, op=ALU.mult),
                             reads=[("gath", pq), ("fe", d)], writes=[("fe", d)])
                    S.op("dve", I("tensor_tensor_scan", out=FOLD[0], data0=PE_[0], data1=FE_[0],
                                  initial=Uq[0][:, CTX - 1:CTX], op0=ALU.mult, op1=ALU.add),
                         reads=[("pe", 0), ("fe", 0), ("fold", 0)] + UQ(0, [0]), writes=[("fold", 0)])
                    S.op("dve", I("tensor_tensor_scan", out=FOLD[1][:, ::-1], data0=PE_[1][:, ::-1], data1=FE_[1][:, ::-1],
                                  initial=Uq[1][:, 0:1], op0=ALU.mult, op1=ALU.add),
                         reads=[("pe", 1), ("fe", 1), ("fold", 1)] + UQ(1, [0]), writes=[("fold", 1)])
                    S.op("dve", I("tensor_tensor_scan", out=Uq[0][:, CTX:NT], data0=Aq[0][:, CTX:NT], data1=Uq[0][:, CTX:NT],
                                  initial=FOLD[0][:, 7:8], op0=ALU.mult, op1=ALU.add),
                         reads=AQ(0, lat) + UQ(0, lat) + [("fold", 0)], writes=UQ(0, lat))
                    S.op("dve", I("tensor_tensor_scan", out=Uq[1][:, CTX:NT][:, ::-1], data0=Aq[1][:, CTX:NT][:, ::-1],
                                  data1=Uq[1][:, CTX:NT][:, ::-1], initial=FOLD[1][:, 0:1], op0=ALU.mult, op1=ALU.add),
                         reads=AQ(1, lat) + UQ(1, lat) + [("fold", 1)], writes=UQ(1, lat))
                    for ti, (t0, N, RL) in ytl:
                        S.op("dve", I("tensor_tensor", out=Uq[0][:, t0:t0 + N], in0=Uq[0][:, t0:t0 + N], in1=Uq[1][:, t0:t0 + N],
                                      op=ALU.add),
                             reads=[("U", pq, 0, ti), ("U", pq, 1, ti)], writes=[("U", pq, 0, ti)])
                    for ti, (t0, N, RL) in ytl:
                        S.op("dve", I("scalar_tensor_tensor", out=YB[:, jj, t0:t0 + N], in0=JUNK[:, t0:t0 + N], scalar=0.5,
                                      in1=Uq[0][:, t0:t0 + N], op0=ALU.mult, op1=ALU.mult),
                             reads=[("junk", ti), ("U", pq, 0, ti)], writes=[("yb", jj, ti)])
                if b < 8:
                    JK = [("junk", ti) for ti in lat]
                    S.op("dve", I("tensor_tensor_scan", out=Ud[0][:, 0:CTX], data0=Ad[0][:, 0:CTX], data1=Ud[0][:, 0:CTX],
                                  initial=vcol("zero"), op0=ALU.mult, op1=ALU.add),
                         reads=AK(0, [0]) + UK(0, [0]), writes=UK(0, [0]))
                    S.op("dve", I("tensor_tensor_scan", out=Ud[1][:, 0:CTX][:, ::-1], data0=Ad[1][:, 0:CTX][:, ::-1],
                                  data1=Ud[1][:, 0:CTX][:, ::-1], initial=vcol("zero"), op0=ALU.mult, op1=ALU.add),
                         reads=AK(1, [0]) + UK(1, [0]), writes=UK(1, [0]))
                    S.op("dve", I("tensor_tensor_scan", out=JUNK[:, CTX:NT], data0=Ad[0][:, CTX:NT], data1=Ud[0][:, CTX:NT],
                                  initial=vcol("zero"), op0=ALU.mult, op1=ALU.add),
                         reads=AK(0, lat) + UK(0, lat) + JK, writes=JK)
                    S.op("dve", I("tensor_copy", out=PKS[pj][:, 0:1], in_=JUNK[:, NT - 1:NT]),
                         reads=JK + [("pk", pj)], writes=[("pk", pj)])
                    S.op("dve", I("tensor_tensor_scan", out=JUNK[:, CTX:NT][:, ::-1], data0=Ad[1][:, CTX:NT][:, ::-1],
                                  data1=Ud[1][:, CTX:NT][:, ::-1], initial=vcol("zero"), op0=ALU.mult, op1=ALU.add),
                         reads=AK(1, lat) + UK(1, lat) + JK, writes=JK)
                    S.op("dve", I("tensor_copy", out=PKS[pj][:, 1:2], in_=JUNK[:, CTX:CTX + 1]),
                         reads=JK + [("pk", pj)], writes=[("pk", pj)])
                    for d in range(2):
                        S.op("act", I("activation", out=JUNK4, in_=RACC[d], func=AF.Identity, accum_out=RSUM[d]),
                             reads=[("racc", d, ti) for ti in lat] + [("junk4",), ("rsum", d)], writes=[("rsum", d), ("junk4",)])
                        S.op("act", I("activation", out=PKS[pj][:, 2 + d:3 + d], in_=RSUM[d], func=AF.Exp,
                                      scale=hc1(d, b), bias=hc1n(d, b)),
                             reads=[("rsum", d), ("pk", pj)], writes=[("pk", pj)])
                    S.dma("pool", I("dma_start", out=cc_in[l][b].ap(), in_=PKS[pj]),
                          reads=[("pk", pj)], writes=[("ccin", l, b)], sem=f"pk{pj}")
                    S.dma("pool", I("collective_compute", kind="AllGather", op=ALU.bypass,
                                    replica_groups=[list(range(NCORES))],
                                    ins=[cc_in[l][b].ap().opt()], outs=[cc_out[l][b].ap().opt()]),
                          reads=[("ccin", l, b)], writes=[("ccout", l, b)], sem="cc", inc=1)
                    S.dma("pool", I("dma_start", out=G3S[pj], in_=cc_out[l][b].ap().rearrange("(r p) f -> p r f", p=128)),
                          reads=[("ccout", l, b)], writes=[("gath", pj)], sem=f"pk{pj}")

        def stage_conv(l, last):
            for j in range(8):
                sv = wacquire(("in", l, 0, j))
                sc = wacquire(("in", l, 2, j))
                sb = wacquire(("in", l, 1, j))
                sz_ = wacquire(("in", l, 3, j))
                w3 = lambda t: vcol(("w3", l), t * 8 + j)
                for ti, (t0, N, RL) in enumerate(TILES):
                    if last and ti == 0:
                        continue
                    rhs = (lambda kc, t0=t0, N=N: H[:, kc, t0:t0 + N])
                    bv, bc, bb, bz = psalloc(), psalloc(), psalloc(), psalloc()
                    mm8(bv, N, sv, rhs, hkeys(ti))
                    mm8(bc, N, sc, rhs, hkeys(ti))
                    mm8(bb, N, sb, rhs, hkeys(ti))
                    mm8(bz, N, sz_, rhs, hkeys(ti))
                    tv, tg, tc, ts, tb_ = talloc(), talloc(), talloc(), talloc(), talloc()
                    S.op("act", I("activation", out=tap(tv, N), in_=PS[:, bv, 0:N], func=AF.Copy),
                         reads=[("ps", bv)], writes=[("t", tv)])
                    S.op("act", I("activation", out=tap(tb_, N), in_=PS[:, bb, 0:N], func=AF.Copy),
                         reads=[("ps", bb)], writes=[("t", tb_)])
                    S.op("act", I("activation", out=tap(ts, N), in_=PS[:, bz, 0:N], func=AF.Silu),
                         reads=[("ps", bz)], writes=[("t", ts)])
                    S.op("dve", I("tensor_tensor", out=tap(tg, N), in0=PS[:, bc, 0:N], in1=tap(tv, N), op=ALU.mult),
                         reads=[("ps", bc), ("t", tv)], writes=[("t", tg)])
                    S.op("act", I("activation", out=tap(tc, N), in_=tap(tg, N), func=AF.Identity, scale=w3(1)),
                         reads=[("t", tg)], writes=[("t", tc)])
                    g3 = tap(tg, N).rearrange("p (r w) -> p r w", w=RL)
                    c3 = tap(tc, N).rearrange("p (r w) -> p r w", w=RL)
                    for tapi, (osl, isl) in ((0, (slice(1, RL), slice(0, RL - 1))),
                                             (2, (slice(0, RL - 1), slice(1, RL)))):
                        S.op("dve", I("scalar_tensor_tensor", out=c3[:, :, osl], in0=g3[:, :, isl], scalar=w3(tapi),
                                      in1=c3[:, :, osl], op0=ALU.mult, op1=ALU.add),
                             reads=[("t", tg), ("t", tc)], writes=[("t", tc)])
                    S.op("dve", I("tensor_tensor", out=tap(tc, N), in0=tap(tc, N), in1=tap(tb_, N), op=ALU.mult),
                         reads=[("t", tb_), ("t", tc)], writes=[("t", tc)])
                    S.op("dve", I("tensor_tensor", out=YA[:, j, t0:t0 + N], in0=tap(tc, N), in1=tap(ts, N), op=ALU.mult),
                         reads=[("t", tc), ("t", ts)], writes=[("ya", j, ti)])
                wrelease(4)

        def stage_merge(l, last):
            for jo in range(8):
                s6 = wacquire(("in", l, 6, jo))
                s7 = wacquire(("in", l, 7, jo))
                sa = wacquire(("oa", l, jo))
                sb = wacquire(("ob", l, jo))
                for ti, (t0, N, RL) in enumerate(TILES):
                    if last and ti == 0:
                        continue
                    bma, bmb, bpa, bpb = psalloc(), psalloc(), psalloc(), psalloc()
                    mm8(bma, N, s6, (lambda kc, t0=t0, N=N: H[:, kc, t0:t0 + N]), hkeys(ti))
                    mm8(bmb, N, s7, (lambda kc, t0=t0, N=N: H[:, kc, t0:t0 + N]), hkeys(ti))
                    mm8(bpa, N, sa, (lambda kc, t0=t0, N=N: YA[:, kc, t0:t0 + N]), [("ya", kc, ti) for kc in range(8)])
                    mm8(bpb, N, sb, (lambda kc, t0=t0, N=N: YB[:, kc, t0:t0 + N]), [("yb", kc, ti) for kc in range(8)])
                    ta, tb = talloc(), talloc()
                    S.op("act", (I("activation", out=tap(ta, N), in_=PS[:, bma, 0:N], func=AF.Sigmoid)),
                         reads=[("ps", bma)], writes=[("t", ta)])
                    S.op("act", (I("activation", out=tap(tb, N), in_=PS[:, bmb, 0:N], func=AF.Sigmoid)),
                         reads=[("ps", bmb)], writes=[("t", tb)])
                    S.op("dve", (I("tensor_tensor",
                        out=tap(ta, N), in0=PS[:, bpa, 0:N], in1=tap(ta, N), op=ALU.mult)),
                        reads=[("ps", bpa), ("t", ta)], writes=[("t", ta)])
                    S.op("dve", (I("tensor_tensor",
                        out=tap(tb, N), in0=PS[:, bpb, 0:N], in1=tap(tb, N), op=ALU.mult)),
                        reads=[("ps", bpb), ("t", tb)], writes=[("t", tb)])
                    S.op("dve", (I("tensor_tensor",
                        out=MG[:, jo, t0:t0 + N], in0=tap(ta, N), in1=tap(tb, N), op=ALU.add)),
                        reads=[("t", ta), ("t", tb)], writes=[("mg", jo, ti)])
                wrelease(4)

        def stage_out(l, last):
            src = xT if l == 0 else xs
            for jo in range(8):
                S.dma("sync", (I("dma_start", out=X[:, jo, :], in_=src[jo * 128:(jo + 1) * 128, :])),
                      reads=[("xs", jo)], writes=[("X", jo, ti) for ti in range(5)], sem=f"x{jo}")
            for jo in range(8):
                so = wacquire(("o", l, jo))
                for ti, (t0, N, RL) in enumerate(TILES):
                    if last and ti == 0:
                        continue
                    n = 1 if ti == 0 else 0
                    bo = psalloc()
                    mm8(bo, N, so, (lambda kc, t0=t0, N=N: MG[:, kc, t0:t0 + N]), [("mg", kc, ti) for kc in range(8)])
                    S.op("dve", (I("scalar_tensor_tensor",
                        out=X[:, jo, t0:t0 + N], in0=PS[:, bo, 0:N], scalar=mod_col(l, 2, jo, n),
                        in1=X[:, jo, t0:t0 + N], op0=ALU.mult, op1=ALU.add)),
                        reads=[("ps", bo), ("X", jo, ti)], writes=[("X", jo, ti)])
                if not last:
                    S.dma("act", (I("dma_start", out=xs[jo * 128:(jo + 1) * 128, :], in_=X[:, jo, :])),
                          reads=[("X", jo, ti) for ti in range(5)], writes=[("xs", jo)], sem=f"x{jo}")
                wrelease(1)

        for l in range(depth):
            last = (l == DEPTH - 1)
            stage_norm(l, False)
            S.barrier()
            stage_lru(l, last)
            S.barrier()
            stage_conv(l, last)
            stage_merge(l, last)
            S.barrier()
            stage_out(l, last)
        if final_norm:
            stage_norm(None, True)
        for kc in range(8):
            S.dma("sync", (I("dma_start", out=outT[kc * 128:(kc + 1) * 128, :], in_=X[:, kc, CTX:NT])),
                  reads=[("X", kc, ti) for ti in range(1, 5)], writes=[("out", kc)], sem="out")
        S.barrier()

        sem_names = ["pe", "act", "dve", "pool"] + sorted(S.dcum.keys())
        sems = {n: es.enter_context(nc.semaphore("s_" + n)) for n in sem_names}
        block = es.enter_context(nc.Block())

        def emit(eng_name):
            def body(e):
                for waits, fn, inc in S.q[eng_name]:
                    for s, v in waits:
                        e.wait_ge(sems[s], v)
                    if fn is None:
                        continue
                    ins = None
                    for m, kw in fn:
                        ins = getattr(e, m)(**kw)
                    ins.then_inc(sems[inc[0]], inc[1])
            return body

        block.tensor(emit("pe"))
        block.scalar(emit("act"))
        block.vector(emit("dve"))
        block.gpsimd(emit("pool"))
        block.sync(emit("sync"))
    return nc


_NC_CACHE = {}


def _prep_inputs(inputs):
    f = lambda k: np.asarray(inputs[k], dtype=np.float32)
    x, c, ctx, c_ctx = f("x"), f("c"), f("ctx"), f("c_ctx")
    shared = {k: np.ascontiguousarray(f(k)) for k in ("w_in", "lru_wr", "lru_wi", "w_out_a", "w_out_b", "w_o")}
    per_layer = []
    for l in range(DEPTH):
        per_layer += [_lay(f("norm_g")[l]), _lay(f("b_ada")[l].reshape(3, D)), _lay(f("conv3_w")[l]),
                      _lay(f("conv4_w")[l]), _lay(f("conv4_b")[l]), _lay(f("lru_br")[l]), _lay(f("lru_bi")[l]),
                      _lay(f("lru_lambda")[l])]
    in_maps = []
    for r in range(NCORES):
        b, q = divmod(r, QPB)
        xt = np.concatenate([ctx[b], x[b, q * LAT:(q + 1) * LAT]], axis=0)
        xT = np.ascontiguousarray(xt.T)
        maskf = np.zeros((128, NCORES), np.float32)
        maskb = np.zeros((128, NCORES), np.float32)
        for rr in range(NCORES):
            if rr // QPB == b and rr < r:
                maskf[:, rr] = 1.0
            if rr // QPB == b and rr > r:
                maskb[:, rr] = 1.0
        consts = np.zeros((128, 6), np.float32)
        consts[:, 0] = 1.0
        consts[:, 1] = RMS_EPS
        consts[:, 3] = 0.25
        consts[:, 4 + b] = 1.0
        vec = np.concatenate(per_layer + [_lay(f("final_g")), _lay(c[0]), _lay(c[1]), _lay(c_ctx), maskf, maskb, consts],
                             axis=1)
        assert vec.shape == (128, NV), vec.shape
        m = {"xT": xT, "vec": np.ascontiguousarray(vec),
             "w_ada_s": np.ascontiguousarray(f("w_ada")[:, :, r * 384:(r + 1) * 384])}
        m.update(shared)
        in_maps.append(m)
    return in_maps


def kernel(**inputs):
    if "nc" not in _NC_CACHE:
        _NC_CACHE["nc"] = build_nc()
    nc = _NC_CACHE["nc"]
    in_maps = _prep_inputs(inputs)
    res = run_bass_kernel_spmd(nc, in_maps, core_ids=list(range(NCORES)))
    out = np.empty((NB, SEQ, D), np.float32)
    for r in range(NCORES):
        b, q = divmod(r, QPB)
        out[b, q * LAT:(q + 1) * LAT, :] = np.asarray(res.results[r]["outT"]).T
    return out
```

```python
import contextlib
import numpy as np
import concourse.bass as bass
import concourse.mybir as mybir
from concourse.bass_utils import run_bass_kernel_spmd

F32 = mybir.dt.float32
BF16 = mybir.dt.bfloat16
AF = mybir.ActivationFunctionType
ALU = mybir.AluOpType

D = 1024
NB = 2
SEQ = 8192
DEPTH = 2
CTX = 256
GW = 64
NCORES = 8
QPB = 4
LAT = SEQ // QPB
NT = CTX + LAT
KC = 8
RMS_EPS = 1e-6
TILES = [(0, 256, 256)] + [(CTX + 512 * i, 512, GW) for i in range(4)]
NSLOT = 8
NTEMP = 10

_VL = {}
_off = 0
for _l in range(DEPTH):
    for _name, _n in (("g", 8), ("bada", 24), ("w3", 24), ("w4", 32), ("b4", 8), ("br", 16), ("bi", 16), ("lam", 16)):
        _VL[(_name, _l)] = _off
        _off += _n
for _name, _n in (("fg", 8), ("c0", 8), ("c1", 8), ("cctx", 8), ("maskf", 8), ("maskb", 8), ("one", 1), ("eps", 1),
                  ("zero", 1), ("quarter", 1), ("sel0", 1), ("sel1", 1)):
    _VL[_name] = _off
    _off += _n
NV = _off

LD = 128
DER_SCC = DEPTH * LD
NDER = DER_SCC + 24


def _lay(v):
    v = np.asarray(v, dtype=np.float32).reshape(-1, 8, 128)
    return np.ascontiguousarray(v.transpose(2, 0, 1).reshape(128, -1))


def I(method, **kw):
    return [(method, kw)]


class Sched:
    ENGS = ("pe", "act", "dve", "pool", "sync")

    def __init__(self):
        self.q = {e: [] for e in self.ENGS}
        self.cnt = {e: 0 for e in self.ENGS}
        self.lw = {}
        self.rd = {}
        self.seen = {e: {} for e in self.ENGS}
        self.dcum = {}
        self.same_engine_sync = True
        self.unread = set()
        self.wcount = {}

    def _waits(self, eng, reads, writes, prune_far=False):
        toks = []
        for k in reads:
            t = self.lw.get(k)
            if t is not None:
                toks.append(t)
        for k in writes:
            t = self.lw.get(k)
            if t is not None:
                toks.append(t)
            toks.extend(self.rd.get(k, ()))
        need = {}
        for s, v in toks:
            if s == eng and (eng == "pe" or not self.same_engine_sync):
                continue
            if s == eng and prune_far and self.cnt[eng] + 1 - v >= 8:
                continue
            if need.get(s, 0) < v:
                need[s] = v
        out = []
        for s, v in need.items():
            if self.seen[eng].get(s, 0) >= v:
                continue
            self.seen[eng][s] = v
            out.append((s, v))
        return out

    def _commit(self, tok, reads, writes):
        for k in reads:
            self.unread.discard(k)
        for k in writes:
            self.unread.add(k)
            self.wcount[k] = self.wcount.get(k, 0) + 1
        for k in writes:
            self.lw[k] = tok
            self.rd[k] = []
        for k in reads:
            if k not in writes:
                self.rd.setdefault(k, []).append(tok)

    def op(self, eng, fn, reads=(), writes=()):
        reads, writes = list(reads), list(writes)
        w = self._waits(eng, reads, writes, prune_far=True)
        self.cnt[eng] += 1
        tok = (eng, self.cnt[eng])
        self.q[eng].append((w, fn, (eng, 1)))
        self._commit(tok, reads, writes)
        return tok

    def dma(self, queue, fn, reads, writes, sem, n=1, inc=16):
        reads, writes = list(reads), list(writes)
        w = self._waits(queue, reads, writes)
        self.dcum[sem] = self.dcum.get(sem, 0) + inc * n
        tok = (sem, self.dcum[sem])
        self.q[queue].append((w, fn, (sem, inc)))
        self._commit(tok, reads, writes)
        return tok

    def barrier(self):
        toks = [(e, c) for e, c in self.cnt.items() if c > 0] + list(self.dcum.items())
        for eng in self.ENGS:
            w = []
            for s, v in toks:
                if s == eng:
                    continue
                if self.seen[eng].get(s, 0) >= v:
                    continue
                self.seen[eng][s] = v
                w.append((s, v))
            if w:
                self.q[eng].append((w, None, None))
        self.lw.clear()
        self.rd.clear()
        self.unread.clear()


def build_nc(depth=DEPTH, final_norm=True):
    nc = bass.Bass("TRN2", target_bir_lowering=False)
    S = Sched()

    xT = nc.dram_tensor("xT", [D, NT], F32, kind="ExternalInput").ap()
    vec_d = nc.dram_tensor("vec", [128, NV], F32, kind="ExternalInput").ap()
    w_ada_s = nc.dram_tensor("w_ada_s", [DEPTH, D, 384], F32, kind="ExternalInput").ap()
    w_in_d = nc.dram_tensor("w_in", [DEPTH, D, 8 * D], F32, kind="ExternalInput").ap()
    wr_d = nc.dram_tensor("lru_wr", [DEPTH, 2, 8, 128, 128], F32, kind="ExternalInput").ap()
    wi_d = nc.dram_tensor("lru_wi", [DEPTH, 2, 8, 128, 128], F32, kind="ExternalInput").ap()
    woa_d = nc.dram_tensor("w_out_a", [DEPTH, D, D], F32, kind="ExternalInput").ap()
    wob_d = nc.dram_tensor("w_out_b", [DEPTH, D, D], F32, kind="ExternalInput").ap()
    wo_d = nc.dram_tensor("w_o", [DEPTH, D, D], F32, kind="ExternalInput").ap()
    outT = nc.dram_tensor("outT", [D, LAT], F32, kind="ExternalOutput").ap()
    xs = nc.dram_tensor("xs", [D, NT], F32).ap()
    cc_ada_in = nc.dram_tensor("cc_ada_in", [128, 18], F32)
    cc_ada_out = nc.dram_tensor("cc_ada_out", [NCORES * 128, 18], F32)
    cc_in = [[nc.dram_tensor(f"ccin_{l}_{j}", [128, 4], F32) for j in range(8)] for l in range(DEPTH)]
    cc_out = [[nc.dram_tensor(f"ccout_{l}_{j}", [NCORES * 128, 4], F32) for j in range(8)] for l in range(DEPTH)]

    es = contextlib.ExitStack()
    with es:
        REG_H = es.enter_context(nc.sbuf_tensor("reg_h", [128, 8 * NT], BF16))
        REG_M = es.enter_context(nc.sbuf_tensor("reg_m", [128, 8 * NT], BF16))
        BIG = es.enter_context(nc.sbuf_tensor("reg_big", [128, 21888], F32))
        WR = es.enter_context(nc.sbuf_tensor("wring", [128, NSLOT * 1024], BF16))
        TMP = es.enter_context(nc.sbuf_tensor("tmps", [128, NTEMP * 512], F32))
        VEC = es.enter_context(nc.sbuf_tensor("vecs", [128, NV], F32))
        DER = es.enter_context(nc.sbuf_tensor("der", [128, NDER], F32))
        SM = es.enter_context(nc.sbuf_tensor("sm", [128, 256], F32))
        ADG = es.enter_context(nc.sbuf_tensor("adg", [128, NCORES * 18], F32))
        JUNK = es.enter_context(nc.sbuf_tensor("junk", [128, NT], F32))
        ONES = es.enter_context(nc.sbuf_tensor("ones", [128, 128], F32))
        PS = es.enter_context(nc.psum_tensor("ps", [128, 8, 512], F32))

        H = REG_H[:, :].rearrange("p (k t) -> p k t", k=8)
        MG = REG_M[:, :].rearrange("p (k t) -> p k t", k=8)
        MF = REG_M[:, :].bitcast(F32)
        ADAST = MF[:, 0:6144].rearrange("p (l k n) -> p l k n", l=DEPTH, k=8)
        X = BIG[:, 0:8 * NT].rearrange("p (k t) -> p k t", k=8)
        YB = BIG[:, 0:9216].bitcast(BF16).rearrange("p (k t) -> p k t", k=8)
        YA = BIG[:, 9216:18432].bitcast(BF16).rearrange("p (k t) -> p k t", k=8)
        A_ = [BIG[:, 9216:11520], BIG[:, 11520:13824]]
        U_ = [BIG[:, 13824:16128], BIG[:, 16128:18432]]
        XC = BIG[:, 18432:20736]
        XCB = BIG[:, 20736:21888].bitcast(BF16)
        WS = WR[:, :].rearrange("p (s k m) -> p s k m", s=NSLOT, k=8)

        def vcol(name, i=0, n=1):
            o = _VL[name] + i
            return VEC[:, o:o + n]

        def dcol(o, n=1):
            return DER[:, o:o + n]

        def mod_col(l, which, kc, n):
            return dcol(l * LD + n * 24 + which * 8 + kc)

        def gs_col(l, kc, n):
            return dcol(l * LD + 48 + n * 8 + kc)

        def c1_col(l, d, j):
            return dcol(l * LD + 64 + d * 8 + j)

        def c2_col(l, d, j):
            return dcol(l * LD + 80 + d * 8 + j)

        PKS = [SM[:, 0:4], SM[:, 202:206]]
        G3S = [SM[:, 4:36].rearrange("p (r f) -> p r f", f=4), SM[:, 206:238].rearrange("p (r f) -> p r f", f=4)]
        AD18 = SM[:, 160:178]
        T24 = SM[:, 178:202]
        LSETS = [dict(A=[BIG[:, 9216:11520], BIG[:, 11520:13824]], U=[BIG[:, 13824:16128], BIG[:, 16128:18432]]),
                 dict(A=[MF[:, 0:2304], MF[:, 2304:4608]], U=[MF[:, 4608:6912], MF[:, 6912:9216]])]
        PE_ = [SM[:, 36:44], SM[:, 52:60]]
        FE_ = [SM[:, 44:52], SM[:, 60:68]]
        FOLD = [SM[:, 68:76], SM[:, 76:84]]
        RACC = [SM[:, 84:88], SM[:, 88:92]]
        RSUM = [SM[:, 92:93], SM[:, 93:94]]
        JUNK4 = SM[:, 94:98]
        T8 = SM[:, 102:110]
        T16A = SM[:, 110:126]
        T16B = SM[:, 126:142]

        st = {"ps": 0, "t": 0}

        resv = {}
        protect = set()

        def _alloc(kind, n):
            for i in range(n):
                b = (st[kind] + i) % n
                key = (kind, b)
                if key in S.unread or key in protect:
                    continue
                if key in resv and S.wcount.get(key, 0) <= resv[key]:
                    continue
                resv[key] = S.wcount.get(key, 0)
                st[kind] = (b + 1) % n
                return b
            raise RuntimeError(f"no free {kind} buffer: every one holds a value with no emitted reader")

        def psalloc():
            return _alloc("ps", 8)

        def talloc():
            return _alloc("t", NTEMP)

        def tap(s, N):
            return TMP[:, s * 512:s * 512 + N]

        wplan = []
        for l in range(depth):
            wplan += [("in", l, 4, 0), ("gate", l, 0)]
            for j in range(9):
                if j > 0:
                    wplan += [("in", l, 5, j - 1)]
                if j + 1 < 8:
                    wplan += [("in", l, 4, j + 1), ("gate", l, j + 1)]
            for j in range(8):
                wplan += [("in", l, 0, j), ("in", l, 2, j), ("in", l, 1, j), ("in", l, 3, j)]
            for j in range(8):
                wplan += [("in", l, 6, j), ("in", l, 7, j), ("oa", l, j), ("ob", l, j)]
            for j in range(8):
                wplan += [("o", l, j)]
        wst = {"issued": 0, "consumed": 0, "released": 0}

        def issue_wload(idx):
            spec = wplan[idx]
            slot = idx % NSLOT
            kind = spec[0]
            if kind == "gate":
                _, l, j = spec
                srcs = [(WS[:, slot, 0:2, :], wr_d[l, :, j].rearrange("d k m -> k d m")),
                        (WS[:, slot, 2:4, :], wi_d[l, :, j].rearrange("d k m -> k d m"))]
            else:
                if kind == "in":
                    _, l, c, j = spec
                    src = w_in_d[l].rearrange("(kc p) n -> p kc n", p=128)[:, :, c * D + j * 128:c * D + (j + 1) * 128]
                else:
                    _, l, j = spec
                    base = {"oa": woa_d, "ob": wob_d, "o": wo_d}[kind]
                    src = base[l].rearrange("(kc p) n -> p kc n", p=128)[:, :, j * 128:(j + 1) * 128]
                srcs = [(WS[:, slot, :, :], src)]
            sname = f"w{slot}_{(idx // NSLOT) % 2}"
            for i, (dst, src) in enumerate(srcs):
                S.dma("pool", (I("dma_start", out=dst, in_=src)),
                      reads=[], writes=[("w", slot)] if i == 0 else [], sem=sname)
            if len(srcs) > 1:
                S.lw[("w", slot)] = (sname, S.dcum[sname])

        def wpump():
            lim = min(len(wplan), wst["released"] + NSLOT)
            while wst["issued"] < lim:
                issue_wload(wst["issued"])
                wst["issued"] += 1

        def wacquire(spec):
            c = wst["consumed"]
            assert wplan[c] == spec, (wplan[c], spec)
            assert c < wst["released"] + NSLOT
            wpump()
            wst["consumed"] = c + 1
            return c % NSLOT

        def wrelease(n):
            wst["released"] += n
            assert wst["released"] <= wst["consumed"]
            wpump()

        def mm8(bank, N, slot, rhs_fn, rkeys):
            ins = []
            for kc in range(8):
                ins += I("matmul", out=PS[:, bank, 0:N], lhsT=WS[:, slot, kc, :], rhs=rhs_fn(kc),
                         start=(kc == 0), stop=(kc == 7))
            S.op("pe", ins, reads=[("w", slot)] + rkeys, writes=[("ps", bank)])

        def norm_stats(final):
            out = {}
            for ti, (t0, N, RL) in enumerate(TILES):
                if final and ti == 0:
                    continue
                bank = psalloc()
                for kc in range(8):
                    s = talloc()
                    S.op("act", (I("activation",
                        out=tap(s, N), in_=X[:, kc, t0:t0 + N], func=AF.Square)),
                        reads=[("X", kc, ti)], writes=[("t", s)])
                    S.op("pe", (I("matmul",
                        out=PS[:, bank, 0:N], lhsT=ONES[:, :], rhs=tap(s, N), start=(kc == 0), stop=(kc == 7))),
                        reads=[("t", s), ("ones",)], writes=[("ps", bank)])
                sd = talloc()
                S.op("act", (I("activation",
                    out=tap(sd, N), in_=PS[:, bank, 0:N], func=AF.Sqrt, scale=1.0 / D, bias=vcol("eps"))),
                    reads=[("ps", bank)], writes=[("t", sd)])
                S.op("dve", (I("reciprocal", out=tap(sd, N), in_=tap(sd, N))),
                     reads=[("t", sd)], writes=[("t", sd)])
                protect.add(("t", sd))
                out[ti] = sd
            return out

        S.dma("sync", I("dma_start", out=VEC[:, :], in_=vec_d), reads=[], writes=[("vec",)], sem="vecl")
        S.dma("sync", I("dma_start", out=ADAST, in_=w_ada_s.rearrange("l (kc p) n -> p l kc n", p=128)),
              reads=[], writes=[("adast",)], sem="adal")
        S.op("dve", I("memset", ap=ONES[:, :], constant=1.0), writes=[("ones",)])
        for c in range(8):
            S.dma("sync", I("dma_start", out=X[:, c, :], in_=xT[c * 128:(c + 1) * 128, :]),
                  reads=[], writes=[("X", c, ti) for ti in range(5)], sem=f"x{c}")
        SCC = DER[:, DER_SCC:DER_SCC + 24].rearrange("p (k n) -> p k n", n=3)
        for n, nm in enumerate(("c0", "c1", "cctx")):
            S.op("act", I("activation", out=SCC[:, :, n], in_=vcol(nm, 0, 8), func=AF.Silu),
                 reads=[("vec",)], writes=[("scc", n)])
        fn = []
        for l in range(DEPTH):
            for o in range(3):
                for kc in range(8):
                    fn += I("matmul", out=PS[:, 0, (l * 3 + o) * 3:(l * 3 + o) * 3 + 3],
                            lhsT=ADAST[:, l, kc, o * 128:(o + 1) * 128],
                            rhs=DER[:, DER_SCC + kc * 3:DER_SCC + kc * 3 + 3], start=(kc == 0), stop=(kc == 7))
        S.op("pe", fn, reads=[("adast",)] + [("scc", n) for n in range(3)], writes=[("ps", 0)])
        S.op("dve", I("tensor_copy", out=AD18, in_=PS[:, 0, 0:18]), reads=[("ps", 0)], writes=[("ad18",)])
        S.dma("pool", I("dma_start", out=cc_ada_in.ap(), in_=AD18), reads=[("ad18",)], writes=[("ccadain",)], sem="pk0")
        S.dma("pool", I("collective_compute", kind="AllGather", op=ALU.bypass, replica_groups=[list(range(NCORES))],
                        ins=[cc_ada_in.ap().opt()], outs=[cc_ada_out.ap().opt()]),
              reads=[("ccadain",)], writes=[("ccadaout",)], sem="cc", inc=1)
        S.dma("pool", I("dma_start", out=ADG[:, :].rearrange("p (r f) -> p r f", r=NCORES),
                        in_=cc_ada_out.ap().rearrange("(r p) f -> p r f", p=128)),
              reads=[("ccadaout",)], writes=[("adg",)], sem="pk0")
        for l in range(DEPTH):
            base = l * LD
            S.op("act", I("activation", out=T16A, in_=vcol(("lam", l), 0, 16), func=AF.Exp, scale=-1.0),
                 reads=[("vec",), ("t16a",)], writes=[("t16a",)])
            S.op("dve", I("tensor_scalar", out=T16B, in0=T16A, scalar1=-0.25, scalar2=1.0 / 3.0, op0=ALU.mult, op1=ALU.add),
                 reads=[("t16a",), ("t16b",)], writes=[("t16b",)])
            for cst in (-0.5, 1.0):
                S.op("dve", I("tensor_tensor", out=T16B, in0=T16B, in1=T16A, op=ALU.mult),
                     reads=[("t16a",), ("t16b",)], writes=[("t16b",)])
                S.op("dve", I("tensor_scalar", out=T16B, in0=T16B, scalar1=cst, scalar2=None, op0=ALU.add),
                     reads=[("t16b",)], writes=[("t16b",)])
            S.op("dve", I("tensor_tensor", out=T16B, in0=T16B, in1=T16A, op=ALU.mult),
                 reads=[("t16a",), ("t16b",)], writes=[("t16b",)])
            S.op("dve", I("tensor_scalar", out=DER[:, base + 64:base + 80], in0=T16B, scalar1=-4.0, scalar2=None, op0=ALU.mult),
                 reads=[("t16b",)], writes=[("hc1", l)])
            S.op("dve", I("tensor_scalar", out=DER[:, base + 80:base + 96], in0=T16B, scalar1=-4.0 * LAT, scalar2=None,
                          op0=ALU.mult), reads=[("t16b",)], writes=[("hc1n", l)])
            S.op("dve", I("tensor_scalar", out=DER[:, base + 96:base + 112], in0=vcol(("br", l), 0, 16), scalar1=0.5,
                          scalar2=None, op0=ALU.mult), reads=[("vec",)], writes=[("hbr", l)])
            S.op("dve", I("tensor_scalar", out=DER[:, base + 112:base + 128], in0=vcol(("bi", l), 0, 16), scalar1=0.5,
                          scalar2=None, op0=ALU.mult), reads=[("vec",)], writes=[("hbi", l)])

        PRE0 = norm_stats(False)
        GV = ADG[:, :].rearrange("p (r l o n) -> p l n r o", r=NCORES, l=DEPTH, o=3, n=3)
        T24v = T24.rearrange("p (r o) -> p r o", o=3)
        for l in range(DEPTH):
            base = l * LD
            bada3 = vcol(("bada", l), 0, 24).rearrange("p (r o) -> p r o", o=3)
            S.op("dve", I("tensor_scalar", out=T24v, in0=GV[:, l, 0], scalar1=vcol("sel0"), scalar2=None, op0=ALU.mult),
                 reads=[("adg",), ("vec",), ("t24",)], writes=[("t24",)])
            S.op("dve", I("scalar_tensor_tensor", out=T24v, in0=GV[:, l, 1], scalar=vcol("sel1"), in1=T24v,
                          op0=ALU.mult, op1=ALU.add), reads=[("adg",), ("t24",)], writes=[("t24",)])
            S.op("dve", I("tensor_tensor", out=DER[:, base:base + 24], in0=T24, in1=vcol(("bada", l), 0, 24), op=ALU.add),
                 reads=[("t24",)], writes=[("mod", l, 0)])
            S.op("dve", I("tensor_tensor", out=DER[:, base + 24:base + 48].rearrange("p (r o) -> p r o", o=3),
                          in0=GV[:, l, 2], in1=bada3, op=ALU.add), reads=[("adg",), ("vec",)], writes=[("mod", l, 1)])
            for n in range(2):
                mb = base + n * 24
                S.op("dve", I("tensor_scalar", out=T8, in0=DER[:, mb + 8:mb + 16], scalar1=1.0, scalar2=None, op0=ALU.add),
                     reads=[("mod", l, n), ("t8",)], writes=[("t8",)])
                S.op("dve", I("tensor_tensor", out=DER[:, base + 48 + n * 8:base + 56 + n * 8], in0=T8,
                              in1=vcol(("g", l), 0, 8), op=ALU.mult), reads=[("t8",), ("vec",)], writes=[("gs", l, n)])
        def stage_norm(l, final, pre=None):
            for ti, (t0, N, RL) in enumerate(TILES):
                if final and ti == 0:
                    continue
                n = 1 if ti == 0 else 0
                if pre is not None:
                    sd = pre[ti]
                else:
                    bank = psalloc()
                    for kc in range(8):
                        s = talloc()
                        S.op("act", (I("activation",
                            out=tap(s, N), in_=X[:, kc, t0:t0 + N], func=AF.Square)),
                            reads=[("X", kc, ti)], writes=[("t", s)])
                        S.op("pe", (I("matmul",
                            out=PS[:, bank, 0:N], lhsT=ONES[:, :], rhs=tap(s, N), start=(kc == 0), stop=(kc == 7))),
                            reads=[("t", s), ("ones",)], writes=[("ps", bank)])
                    sd = talloc()
                    S.op("act", (I("activation",
                        out=tap(sd, N), in_=PS[:, bank, 0:N], func=AF.Sqrt, scale=1.0 / D, bias=vcol("eps"))),
                        reads=[("ps", bank)], writes=[("t", sd)])
                    S.op("dve", (I("reciprocal", out=tap(sd, N), in_=tap(sd, N))),
                         reads=[("t", sd)], writes=[("t", sd)])
                    protect.add(("t", sd))
                for kc in range(8):
                    if final:
                        S.op("dve", (I("scalar_tensor_tensor",
                            out=X[:, kc, t0:t0 + N], in0=X[:, kc, t0:t0 + N], scalar=vcol("fg", kc),
                            in1=tap(sd, N), op0=ALU.mult, op1=ALU.mult)),
                            reads=[("X", kc, ti), ("t", sd)], writes=[("X", kc, ti)])
                        S.dma("sync" if kc % 2 == 0 else "act",
                              I("dma_start", out=outT[kc * 128:(kc + 1) * 128, t0 - CTX:t0 - CTX + N], in_=X[:, kc, t0:t0 + N]),
                              reads=[("X", kc, ti)], writes=[("out", kc, ti)], sem="out" if kc % 2 == 0 else "out2")
                    else:
                        s = talloc()
                        S.op("dve", (I("scalar_tensor_tensor",
                            out=tap(s, N), in0=X[:, kc, t0:t0 + N], scalar=gs_col(l, kc, n), in1=tap(sd, N),
                            op0=ALU.mult, op1=ALU.mult)),
                            reads=[("X", kc, ti), ("t", sd)], writes=[("t", s)])
                        S.op("act", (I("activation",
                            out=H[:, kc, t0:t0 + N], in_=tap(s, N), func=AF.Identity,
                            bias=mod_col(l, 0, kc, n))),
                            reads=[("t", s)], writes=[("h", kc, ti)])
                protect.discard(("t", sd))

        def hkeys(ti):
            return [("h", kc, ti) for kc in range(8)]

        def stage_lru(l, last):
            lat = [1, 2, 3, 4]
            hc1 = lambda d, b: dcol(l * LD + 64 + d * 8 + b)
            hc1n = lambda d, b: dcol(l * LD + 80 + d * 8 + b)
            ytl = [(ti, t) for ti, t in enumerate(TILES) if not (last and ti == 0)]
            gbanks = {}

            def front(b):
                s4 = wacquire(("in", l, 4, b))
                w4 = lambda t: vcol(("w4", l), t * 8 + b)
                bxs = []
                for ti, (t0, N, RL) in enumerate(TILES):
                    bx = psalloc()
                    bxs.append(bx)
                    mm8(bx, N, s4, (lambda kc, t0=t0, N=N: H[:, kc, t0:t0 + N]), hkeys(ti))
                for ti, (t0, N, RL) in enumerate(TILES):
                    S.op("act", I("activation", out=XC[:, t0:t0 + N], in_=PS[:, bxs[ti], 0:N], func=AF.Identity,
                                  scale=w4(2), bias=vcol(("b4", l), b)),
                         reads=[("ps", bxs[ti])], writes=[("XC", ti)])
                for tapi, sh_o, sh_i in ((0, 2, 0), (1, 1, 0), (3, 0, 1)):
                    for ti, (t0, N, RL) in enumerate(TILES):
                        xc3 = XC[:, t0:t0 + N].rearrange("p (r w) -> p r w", w=RL)
                        ps3 = PS[:, bxs[ti], 0:N].rearrange("p (r w) -> p r w", w=RL)
                        ln = RL - max(sh_o, sh_i)
                        osl = slice(sh_o, sh_o + ln)
                        isl = slice(sh_i, sh_i + ln)
                        S.op("dve", I("scalar_tensor_tensor", out=xc3[:, :, osl], in0=ps3[:, :, isl], scalar=w4(tapi),
                                      in1=xc3[:, :, osl], op0=ALU.mult, op1=ALU.add),
                             reads=[("ps", bxs[ti]), ("XC", ti)], writes=[("XC", ti)])
                wrelease(1)

            def back(b):
                sg = wacquire(("gate", l, b))
                for ti, (t0, N, RL) in enumerate(TILES):
                    S.op("act", I("activation", out=XCB[:, t0:t0 + N], in_=XC[:, t0:t0 + N], func=AF.Copy),
                         reads=[("XC", ti)], writes=[("xcb", ti)])
                gslot[b] = sg
                for ti in range(len(TILES)):
                    gate_mm(b, 0, 0, ti)

            gslot = {}

            def gate_mm(b, d, g, ti):
                t0, N, RL = TILES[ti]
                bk = psalloc()
                gbanks[(b, d, g, ti)] = bk
                sg = gslot[b]
                S.op("pe", I("matmul", out=PS[:, bk, 0:N], lhsT=WS[:, sg, 2 * g + d, :], rhs=XCB[:, t0:t0 + N],
                             start=True, stop=True),
                     reads=[("w", sg), ("xcb", ti)], writes=[("ps", bk)])

            GROUPS = [(0, 0), (0, 1), (1, 0), (1, 1)]
            front(0)
            back(0)
            for b in range(9):
                pj = b % 2
                Ad, Ud = LSETS[pj]["A"], LSETS[pj]["U"]
                AK = lambda d, tis, pj=pj: [("A", pj, d, ti) for ti in tis]
                UK = lambda d, tis, pj=pj: [("U", pj, d, ti) for ti in tis]
                if b < 8:
                    for gi, (d, g) in enumerate(GROUPS):
                        nxt = GROUPS[gi + 1] if gi + 1 < 4 else None
                        if nxt:
                            for ti in range(3):
                                gate_mm(b, nxt[0], nxt[1], ti)
                        for ti, (t0, N, RL) in enumerate(TILES):
                            bk = gbanks.pop((b, d, g, ti))
                            hb = dcol(l * LD + (96 if g == 0 else 112) + d * 8 + b)
                            if g == 0:
                                kw = dict(accum_out=RACC[d][:, ti - 1:ti]) if ti > 0 else {}
                                S.op("act", I("activation", out=Ad[d][:, t0:t0 + N], in_=PS[:, bk, 0:N], func=AF.Tanh,
                                              scale=0.5, bias=hb, **kw),
                                     reads=[("ps", bk)], writes=[("A", pj, d, ti)] + ([("racc", d, ti)] if ti > 0 else []))
                            else:
                                S.op("act", I("activation", out=Ud[d][:, t0:t0 + N], in_=PS[:, bk, 0:N], func=AF.Tanh,
                                              scale=0.5, bias=hb),
                                     reads=[("ps", bk)], writes=[("U", pj, d, ti)])
                            if nxt and ti + 3 < len(TILES):
                                gate_mm(b, nxt[0], nxt[1], ti + 3)
                        if g == 1:
                            for ti, (t0, N, RL) in enumerate(TILES):
                                S.op("dve", I("scalar_tensor_tensor", out=Ud[d][:, t0:t0 + N], in0=Ud[d][:, t0:t0 + N],
                                              scalar=1.0, in1=XC[:, t0:t0 + N], op0=ALU.add, op1=ALU.mult),
                                     reads=[("U", pj, d, ti), ("XC", ti)], writes=[("U", pj, d, ti)])
                    wrelease(1)
                if b > 0:
                    jj = b - 1
                    s5 = wacquire(("in", l, 5, jj))
                    bzs, ths = {}, {}
                    for ti, (t0, N, RL) in ytl:
                        bzs[ti] = psalloc()
                        mm8(bzs[ti], N, s5, (lambda kc, t0=t0, N=N: H[:, kc, t0:t0 + N]), hkeys(ti))
                    wrelease(1)
                    for ti, (t0, N, RL) in ytl:
                        ths[ti] = talloc()
                        S.op("act", I("activation", out=tap(ths[ti], N), in_=PS[:, bzs[ti], 0:N], func=AF.Tanh, scale=0.5),
                             reads=[("ps", bzs[ti])], writes=[("t", ths[ti])])
                    for ti, (t0, N, RL) in ytl:
                        S.op("dve", I("scalar_tensor_tensor", out=JUNK[:, t0:t0 + N], in0=tap(ths[ti], N), scalar=1.0,
                                      in1=PS[:, bzs[ti], 0:N], op0=ALU.add, op1=ALU.mult),
                             reads=[("t", ths[ti]), ("ps", bzs[ti])], writes=[("junk", ti)])
                if b + 1 < 8:
                    front(b + 1)
                if b < 8:
                    for d in range(2):
                        for ti, (t0, N, RL) in enumerate(TILES):
                            S.op("act", I("activation", out=Ad[d][:, t0:t0 + N], in_=Ad[d][:, t0:t0 + N], func=AF.Exp,
                                          scale=hc1(d, b), bias=hc1(d, b)),
                                 reads=[("A", pj, d, ti)], writes=[("A", pj, d, ti)])
                if b < 8:
                    for d in range(2):
                        for ti, (t0, N, RL) in enumerate(TILES):
                            tm = talloc()
                            S.op("act", I("activation", out=tap(tm, N), in_=Ad[d][:, t0:t0 + N], func=AF.Square),
                                 reads=[("A", pj, d, ti)], writes=[("t", tm)])
                            S.op("act", I("activation", out=tap(tm, N), in_=tap(tm, N), func=AF.Sqrt, scale=-0.25,
                                          bias=vcol("quarter")), reads=[("t", tm)], writes=[("t", tm)])
                            S.op("dve", I("tensor_tensor", out=Ud[d][:, t0:t0 + N], in0=Ud[d][:, t0:t0 + N], in1=tap(tm, N),
                                          op=ALU.mult),
                                 reads=[("U", pj, d, ti), ("t", tm)], writes=[("U", pj, d, ti)])
                if b + 1 < 8:
                    back(b + 1)
                if b > 0:
                    jj = b - 1
                    pq = jj % 2
                    Aq, Uq = LSETS[pq]["A"], LSETS[pq]["U"]
                    AQ = lambda d, tis, pq=pq: [("A", pq, d, ti) for ti in tis]
                    UQ = lambda d, tis, pq=pq: [("U", pq, d, ti) for ti in tis]
                    G3 = G3S[pq]
                    for d in range(2):
                        msk = vcol("maskf" if d == 0 else "maskb", 0, 8)
                        S.op("dve", I("tensor_scalar", out=PE_[d], in0=G3[:, :, 2 + d], scalar1=-1.0, scalar2=None, op0=ALU.add),
                             reads=[("gath", pq), ("pe", d)], writes=[("pe", d)])
                        S.op("dve", I("tensor_tensor", out=PE_[d], in0=PE_[d], in1=msk, op=ALU.mult),
                             reads=[("pe", d)], writes=[("pe", d)])
                        S.op("dve", I("tensor_scalar", out=PE_[d], in0=PE_[d], scalar1=1.0, scalar2=None, op0=ALU.add),
                             reads=[("pe", d)], writes=[("pe", d)])
                        S.op("dve", I("tensor_tensor", out=FE_[d], in0=G3[:, :, d], in1=msk, op=ALU.mult),
                             reads=[("gath", pq), ("fe", d)], writes=[("fe", d)])
                    S.op("dve", I("tensor_tensor_scan", out=FOLD[0], data0=PE_[0], data1=FE_[0],
                                  initial=Uq[0][:, CTX - 1:CTX], op0=ALU.mult, op1=ALU.add),
                         reads=[("pe", 0), ("fe", 0), ("fold", 0)] + UQ(0, [0]), writes=[("fold", 0)])
                    S.op("dve", I("tensor_tensor_scan", out=FOLD[1][:, ::-1], data0=PE_[1][:, ::-1], data1=FE_[1][:, ::-1],
                                  initial=Uq[1][:, 0:1], op0=ALU.mult, op1=ALU.add),
                         reads=[("pe", 1), ("fe", 1), ("fold", 1)] + UQ(1, [0]), writes=[("fold", 1)])
                    S.op("dve", I("tensor_tensor_scan", out=Uq[0][:, CTX:NT], data0=Aq[0][:, CTX:NT], data1=Uq[0][:, CTX:NT],
                                  initial=FOLD[0][:, 7:8], op0=ALU.mult, op1=ALU.add),
                         reads=AQ(0, lat) + UQ(0, lat) + [("fold", 0)], writes=UQ(0, lat))
                    S.op("dve", I("tensor_tensor_scan", out=Uq[1][:, CTX:NT][:, ::-1], data0=Aq[1][:, CTX:NT][:, ::-1],
                                  data1=Uq[1][:, CTX:NT][:, ::-1], initial=FOLD[1][:, 0:1], op0=ALU.mult, op1=ALU.add),
                         reads=AQ(1, lat) + UQ(1, lat) + [("fold", 1)], writes=UQ(1, lat))
                    for ti, (t0, N, RL) in ytl:
                        S.op("dve", I("tensor_tensor", out=Uq[0][:, t0:t0 + N], in0=Uq[0][:, t0:t0 + N], in1=Uq[1][:, t0:t0 + N],
                                      op=ALU.add),
                             reads=[("U", pq, 0, ti), ("U", pq, 1, ti)], writes=[("U", pq, 0, ti)])
                    for ti, (t0, N, RL) in ytl:
                        S.op("dve", I("scalar_tensor_tensor", out=YB[:, jj, t0:t0 + N], in0=JUNK[:, t0:t0 + N], scalar=0.5,
                                      in1=Uq[0][:, t0:t0 + N], op0=ALU.mult, op1=ALU.mult),
                             reads=[("junk", ti), ("U", pq, 0, ti)], writes=[("yb", jj, ti)])
                if b < 8:
                    JK = [("junk", ti) for ti in lat]
                    for d in range(2):
                        S.op("act", I("activation", out=JUNK4, in_=RACC[d], func=AF.Identity, accum_out=RSUM[d]),
                             reads=[("racc", d, ti) for ti in lat] + [("junk4",), ("rsum", d)], writes=[("rsum", d), ("junk4",)])
                        S.op("act", I("activation", out=PKS[pj][:, 2 + d:3 + d], in_=RSUM[d], func=AF.Exp,
                                      scale=hc1(d, b), bias=hc1n(d, b)),
                             reads=[("rsum", d), ("pkP", pj)], writes=[("pkP", pj)])
                    S.op("dve", I("tensor_tensor_scan", out=Ud[0][:, 0:CTX], data0=Ad[0][:, 0:CTX], data1=Ud[0][:, 0:CTX],
                                  initial=vcol("zero"), op0=ALU.mult, op1=ALU.add),
                         reads=AK(0, [0]) + UK(0, [0]), writes=UK(0, [0]))
                    S.op("dve", I("tensor_tensor_scan", out=Ud[1][:, 0:CTX][:, ::-1], data0=Ad[1][:, 0:CTX][:, ::-1],
                                  data1=Ud[1][:, 0:CTX][:, ::-1], initial=vcol("zero"), op0=ALU.mult, op1=ALU.add),
                         reads=AK(1, [0]) + UK(1, [0]), writes=UK(1, [0]))
                    S.op("dve", I("tensor_tensor_scan", out=JUNK[:, CTX:NT], data0=Ad[0][:, CTX:NT], data1=Ud[0][:, CTX:NT],
                                  initial=vcol("zero"), op0=ALU.mult, op1=ALU.add),
                         reads=AK(0, lat) + UK(0, lat) + JK, writes=JK)
                    S.op("dve", I("tensor_copy", out=PKS[pj][:, 0:1], in_=JUNK[:, NT - 1:NT]),
                         reads=JK + [("pkF", pj)], writes=[("pkF", pj)])
                    S.op("dve", I("tensor_tensor_scan", out=JUNK[:, CTX:NT][:, ::-1], data0=Ad[1][:, CTX:NT][:, ::-1],
                                  data1=Ud[1][:, CTX:NT][:, ::-1], initial=vcol("zero"), op0=ALU.mult, op1=ALU.add),
                         reads=AK(1, lat) + UK(1, lat) + JK, writes=JK)
                    S.op("dve", I("tensor_copy", out=PKS[pj][:, 1:2], in_=JUNK[:, CTX:CTX + 1]),
                         reads=JK + [("pkF", pj)], writes=[("pkF", pj)])
                    S.dma("pool", I("dma_start", out=cc_in[l][b].ap(), in_=PKS[pj]),
                          reads=[("pkF", pj), ("pkP", pj)], writes=[("ccin", l, b)], sem=f"pk{pj}")
                    S.dma("pool", I("collective_compute", kind="AllGather", op=ALU.bypass,
                                    replica_groups=[list(range(NCORES))],
                                    ins=[cc_in[l][b].ap().opt()], outs=[cc_out[l][b].ap().opt()]),
                          reads=[("ccin", l, b)], writes=[("ccout", l, b)], sem="cc", inc=1)
                    S.dma("pool", I("dma_start", out=G3S[pj], in_=cc_out[l][b].ap().rearrange("(r p) f -> p r f", p=128)),
                          reads=[("ccout", l, b)], writes=[("gath", pj)], sem=f"pk{pj}")

        def stage_conv(l, last):
            for j in range(8):
                sv = wacquire(("in", l, 0, j))
                sc = wacquire(("in", l, 2, j))
                sb = wacquire(("in", l, 1, j))
                sz_ = wacquire(("in", l, 3, j))
                w3 = lambda t: vcol(("w3", l), t * 8 + j)
                for ti, (t0, N, RL) in enumerate(TILES):
                    if last and ti == 0:
                        continue
                    rhs = (lambda kc, t0=t0, N=N: H[:, kc, t0:t0 + N])
                    bv, bc, bb, bz = psalloc(), psalloc(), psalloc(), psalloc()
                    mm8(bv, N, sv, rhs, hkeys(ti))
                    mm8(bc, N, sc, rhs, hkeys(ti))
                    mm8(bb, N, sb, rhs, hkeys(ti))
                    mm8(bz, N, sz_, rhs, hkeys(ti))
                    tv, tg, tc, ts, tb_ = talloc(), talloc(), talloc(), talloc(), talloc()
                    S.op("act", I("activation", out=tap(tv, N), in_=PS[:, bv, 0:N], func=AF.Copy),
                         reads=[("ps", bv)], writes=[("t", tv)])
                    S.op("act", I("activation", out=tap(tb_, N), in_=PS[:, bb, 0:N], func=AF.Copy),
                         reads=[("ps", bb)], writes=[("t", tb_)])
                    S.op("act", I("activation", out=tap(ts, N), in_=PS[:, bz, 0:N], func=AF.Silu),
                         reads=[("ps", bz)], writes=[("t", ts)])
                    S.op("dve", I("tensor_tensor", out=tap(tg, N), in0=PS[:, bc, 0:N], in1=tap(tv, N), op=ALU.mult),
                         reads=[("ps", bc), ("t", tv)], writes=[("t", tg)])
                    S.op("act", I("activation", out=tap(tc, N), in_=tap(tg, N), func=AF.Identity, scale=w3(1)),
                         reads=[("t", tg)], writes=[("t", tc)])
                    g3 = tap(tg, N).rearrange("p (r w) -> p r w", w=RL)
                    c3 = tap(tc, N).rearrange("p (r w) -> p r w", w=RL)
                    for tapi, (osl, isl) in ((0, (slice(1, RL), slice(0, RL - 1))),
                                             (2, (slice(0, RL - 1), slice(1, RL)))):
                        S.op("dve", I("scalar_tensor_tensor", out=c3[:, :, osl], in0=g3[:, :, isl], scalar=w3(tapi),
                                      in1=c3[:, :, osl], op0=ALU.mult, op1=ALU.add),
                             reads=[("t", tg), ("t", tc)], writes=[("t", tc)])
                    S.op("dve", I("tensor_tensor", out=tap(tc, N), in0=tap(tc, N), in1=tap(tb_, N), op=ALU.mult),
                         reads=[("t", tb_), ("t", tc)], writes=[("t", tc)])
                    S.op("dve", I("tensor_tensor", out=YA[:, j, t0:t0 + N], in0=tap(tc, N), in1=tap(ts, N), op=ALU.mult),
                         reads=[("t", tc), ("t", ts)], writes=[("ya", j, ti)])
                wrelease(4)

        def stage_merge(l, last):
            for jo in range(8):
                s6 = wacquire(("in", l, 6, jo))
                s7 = wacquire(("in", l, 7, jo))
                sa = wacquire(("oa", l, jo))
                sb = wacquire(("ob", l, jo))
                for ti, (t0, N, RL) in enumerate(TILES):
                    if last and ti == 0:
                        continue
                    bma, bmb, bpa, bpb = psalloc(), psalloc(), psalloc(), psalloc()
                    mm8(bma, N, s6, (lambda kc, t0=t0, N=N: H[:, kc, t0:t0 + N]), hkeys(ti))
                    mm8(bmb, N, s7, (lambda kc, t0=t0, N=N: H[:, kc, t0:t0 + N]), hkeys(ti))
                    mm8(bpa, N, sa, (lambda kc, t0=t0, N=N: YA[:, kc, t0:t0 + N]), [("ya", kc, ti) for kc in range(8)])
                    mm8(bpb, N, sb, (lambda kc, t0=t0, N=N: YB[:, kc, t0:t0 + N]), [("yb", kc, ti) for kc in range(8)])
                    ta, tb = talloc(), talloc()
                    S.op("act", (I("activation", out=tap(ta, N), in_=PS[:, bma, 0:N], func=AF.Sigmoid)),
                         reads=[("ps", bma)], writes=[("t", ta)])
                    S.op("act", (I("activation", out=tap(tb, N), in_=PS[:, bmb, 0:N], func=AF.Sigmoid)),
                         reads=[("ps", bmb)], writes=[("t", tb)])
                    S.op("dve", (I("tensor_tensor",
                        out=tap(ta, N), in0=PS[:, bpa, 0:N], in1=tap(ta, N), op=ALU.mult)),
                        reads=[("ps", bpa), ("t", ta)], writes=[("t", ta)])
                    S.op("dve", (I("tensor_tensor",
                        out=tap(tb, N), in0=PS[:, bpb, 0:N], in1=tap(tb, N), op=ALU.mult)),
                        reads=[("ps", bpb), ("t", tb)], writes=[("t", tb)])
                    S.op("dve", (I("tensor_tensor",
                        out=MG[:, jo, t0:t0 + N], in0=tap(ta, N), in1=tap(tb, N), op=ALU.add)),
                        reads=[("t", ta), ("t", tb)], writes=[("mg", jo, ti)])
                wrelease(4)

        def stage_out(l, last):
            src = xT if l == 0 else xs
            for jo in range(8):
                S.dma("sync", (I("dma_start", out=X[:, jo, :], in_=src[jo * 128:(jo + 1) * 128, :])),
                      reads=[("xs", jo)], writes=[("X", jo, ti) for ti in range(5)], sem=f"x{jo}")
            for jo in range(8):
                so = wacquire(("o", l, jo))
                for ti, (t0, N, RL) in enumerate(TILES):
                    if last and ti == 0:
                        continue
                    n = 1 if ti == 0 else 0
                    bo = psalloc()
                    mm8(bo, N, so, (lambda kc, t0=t0, N=N: MG[:, kc, t0:t0 + N]), [("mg", kc, ti) for kc in range(8)])
                    S.op("dve", (I("scalar_tensor_tensor",
                        out=X[:, jo, t0:t0 + N], in0=PS[:, bo, 0:N], scalar=mod_col(l, 2, jo, n),
                        in1=X[:, jo, t0:t0 + N], op0=ALU.mult, op1=ALU.add)),
                        reads=[("ps", bo), ("X", jo, ti)], writes=[("X", jo, ti)])
                if not last:
                    S.dma("act", (I("dma_start", out=xs[jo * 128:(jo + 1) * 128, :], in_=X[:, jo, :])),
                          reads=[("X", jo, ti) for ti in range(5)], writes=[("xs", jo)], sem=f"x{jo}")
                wrelease(1)

        for l in range(depth):
            last = (l == DEPTH - 1)
            stage_norm(l, False, pre=PRE0 if l == 0 else None)
            S.barrier()
            stage_lru(l, last)
            S.barrier()
            stage_conv(l, last)
            stage_merge(l, last)
            S.barrier()
            stage_out(l, last)
        if final_norm:
            stage_norm(None, True)
        S.barrier()

        sem_names = ["pe", "act", "dve", "pool"] + sorted(S.dcum.keys())
        sems = {n: es.enter_context(nc.semaphore("s_" + n)) for n in sem_names}
        block = es.enter_context(nc.Block())

        def emit(eng_name):
            def body(e):
                for waits, fn, inc in S.q[eng_name]:
                    for s, v in waits:
                        e.wait_ge(sems[s], v)
                    if fn is None:
                        continue
                    ins = None
                    for m, kw in fn:
                        ins = getattr(e, m)(**kw)
                    ins.then_inc(sems[inc[0]], inc[1])
            return body

        block.tensor(emit("pe"))
        block.scalar(emit("act"))
        block.vector(emit("dve"))
        block.gpsimd(emit("pool"))
        block.sync(emit("sync"))
    return nc


_NC_CACHE = {}


def _prep_inputs(inputs):
    f = lambda k: np.asarray(inputs[k], dtype=np.float32)
    x, c, ctx, c_ctx = f("x"), f("c"), f("ctx"), f("c_ctx")
    shared = {k: np.ascontiguousarray(f(k)) for k in ("w_in", "lru_wr", "lru_wi", "w_out_a", "w_out_b", "w_o")}
    per_layer = []
    for l in range(DEPTH):
        per_layer += [_lay(f("norm_g")[l]), _lay(f("b_ada")[l].reshape(3, D)), _lay(f("conv3_w")[l]),
                      _lay(f("conv4_w")[l]), _lay(f("conv4_b")[l]), _lay(f("lru_br")[l]), _lay(f("lru_bi")[l]),
                      _lay(f("lru_lambda")[l])]
    in_maps = []
    for r in range(NCORES):
        b, q = divmod(r, QPB)
        xt = np.concatenate([ctx[b], x[b, q * LAT:(q + 1) * LAT]], axis=0)
        xT = np.ascontiguousarray(xt.T)
        maskf = np.zeros((128, NCORES), np.float32)
        maskb = np.zeros((128, NCORES), np.float32)
        for rr in range(NCORES):
            if rr // QPB == b and rr < r:
                maskf[:, rr] = 1.0
            if rr // QPB == b and rr > r:
                maskb[:, rr] = 1.0
        consts = np.zeros((128, 6), np.float32)
        consts[:, 0] = 1.0
        consts[:, 1] = RMS_EPS
        consts[:, 3] = 0.25
        consts[:, 4 + b] = 1.0
        vec = np.concatenate(per_layer + [_lay(f("final_g")), _lay(c[0]), _lay(c[1]), _lay(c_ctx), maskf, maskb, consts],
                             axis=1)
        assert vec.shape == (128, NV), vec.shape
        m = {"xT": xT, "vec": np.ascontiguousarray(vec),
             "w_ada_s": np.ascontiguousarray(f("w_ada")[:, :, r * 384:(r + 1) * 384])}
        m.update(shared)
        in_maps.append(m)
    return in_maps


def kernel(**inputs):
    if "nc" not in _NC_CACHE:
        _NC_CACHE["nc"] = build_nc()
    nc = _NC_CACHE["nc"]
    in_maps = _prep_inputs(inputs)
    res = run_bass_kernel_spmd(nc, in_maps, core_ids=list(range(NCORES)))
    out = np.empty((NB, SEQ, D), np.float32)
    for r in range(NCORES):
        b, q = divmod(r, QPB)
        out[b, q * LAT:(q + 1) * LAT, :] = np.asarray(res.results[r]["outT"]).T
    return out
```
